# Optimizing a Trainium2 kernel written in Bass

```python
import math
import jax, jax.numpy as jnp
from jax import lax
import numpy as np

D_MODEL = 1024
BATCH = 4
SEQ = 8192
DEPTH = 4

HEAD_DIM = 64
N_HEADS = D_MODEL // HEAD_DIM
H_FOX = N_HEADS // 4
H_DSA = N_HEADS // 4
H_DIL = N_HEADS // 4
H_MOBA = N_HEADS - H_FOX - H_DSA - H_DIL
IDX_HEADS = 8
IDX_DIM = 32
DSA_TOPK = 256
DIL_PATTERNS = ((128, 1), (512, 4), (2048, 16))
MOBA_BLOCK = 256
MOBA_TOPK = 3
Q_BLOCK = 128
MOBA_Q_BLOCK = 64
D_FF = 4 * D_MODEL
ROPE_THETA = 10000.0
LN_EPS = 1e-5
DEEPNORM_ALPHA = (2.0 * DEPTH) ** 0.25
DEEPNORM_BETA = (8.0 * DEPTH) ** -0.25
ATTN_SCALE = HEAD_DIM ** -0.5

SPLIT_SIZES = (
    H_FOX * HEAD_DIM, H_FOX * HEAD_DIM, H_FOX * HEAD_DIM, H_FOX,
    H_DSA * HEAD_DIM, H_DSA * HEAD_DIM, H_DSA * HEAD_DIM,
    IDX_HEADS * IDX_DIM, IDX_DIM, IDX_HEADS,
    H_DIL * HEAD_DIM, H_DIL * HEAD_DIM, H_DIL * HEAD_DIM,
    H_MOBA * HEAD_DIM, H_MOBA * HEAD_DIM, H_MOBA * HEAD_DIM,
)
D_IN = sum(SPLIT_SIZES)

kernel_name = "hymba_style_fox_dsa_dilated_moba_trunk"

F32 = jnp.float32


def layer_norm(x, g, b):
    xf = x.astype(F32)
    mu = jnp.mean(xf, axis=-1, keepdims=True)
    var = jnp.mean(jnp.square(xf - mu), axis=-1, keepdims=True)
    return ((xf - mu) * lax.rsqrt(var + LN_EPS) * g.astype(F32) + b.astype(F32)).astype(x.dtype)


def rope(x, pos):
    d = x.shape[-1]
    half = d // 2
    inv = ROPE_THETA ** (-jnp.arange(half, dtype=F32) / half)
    ang = pos.astype(F32)[:, None] * inv[None, :]
    cos = jnp.cos(ang)[None, :, None, :]
    sin = jnp.sin(ang)[None, :, None, :]
    xf = x.astype(F32)
    x1, x2 = xf[..., :half], xf[..., half:]
    return jnp.concatenate([x1 * cos - x2 * sin, x2 * cos + x1 * sin], axis=-1).astype(x.dtype)


def fox_attention(q, k, v, logf):
    B, T, H, d = q.shape
    c = jnp.cumsum(logf, axis=1)
    c_key = c.transpose(0, 2, 1)
    nb = T // Q_BLOCK
    qb = q.reshape(B, nb, Q_BLOCK, H, d).transpose(1, 0, 2, 3, 4)
    cb = c.reshape(B, nb, Q_BLOCK, H).transpose(1, 0, 3, 2)
    kpos = jnp.arange(T)

    def body(args):
        i, qi, ci = args
        s = jnp.einsum('bqhd,bkhd->bhqk', qi, k, preferred_element_type=F32) * ATTN_SCALE
        s = s + ci[..., None] - c_key[:, :, None, :]
        qpos = i * Q_BLOCK + jnp.arange(Q_BLOCK)
        s = jnp.where((kpos[None, :] <= qpos[:, None])[None, None], s, -jnp.inf)
        p = jax.nn.softmax(s, axis=-1)
        return jnp.einsum('bhqk,bkhd->bqhd', p.astype(v.dtype), v)

    out = lax.map(body, (jnp.arange(nb), qb, cb))
    return out.transpose(1, 0, 2, 3, 4).reshape(B, T, H, d)


def dsa_attention(q, k, v, qi, ki, wi):
    B, T, H, d = q.shape
    topk = min(DSA_TOPK, T // 4)
    nb = T // Q_BLOCK
    kpos = jnp.arange(T)
    bidx = jnp.arange(B)[:, None, None]
    qbs = q.reshape(B, nb, Q_BLOCK, H, d).transpose(1, 0, 2, 3, 4)
    qib = qi.reshape(B, nb, Q_BLOCK, IDX_HEADS, IDX_DIM).transpose(1, 0, 2, 3, 4)
    wib = wi.reshape(B, nb, Q_BLOCK, IDX_HEADS).transpose(1, 0, 2, 3)

    def body(args):
        i, qq, qx, wx = args
        qpos = i * Q_BLOCK + jnp.arange(Q_BLOCK)
        rel = jax.nn.relu(jnp.einsum('bqhe,bke->bqhk', qx, ki, preferred_element_type=F32))
        score = jnp.einsum('bqh,bqhk->bqk', wx.astype(F32), rel)
        score = jnp.where((kpos[None, :] <= qpos[:, None])[None], score, -jnp.inf)
        _, idx = lax.top_k(score, topk)
        valid = idx <= qpos[None, :, None]
        kg = k[bidx, idx]
        vg = v[bidx, idx]
        s = jnp.einsum('bqhd,bqkhd->bhqk', qq, kg, preferred_element_type=F32) * ATTN_SCALE
        s = jnp.where(valid[:, None], s, -jnp.inf)
        p = jax.nn.softmax(s, axis=-1)
        return jnp.einsum('bhqk,bqkhd->bqhd', p.astype(v.dtype), vg)

    out = lax.map(body, (jnp.arange(nb), qbs, qib, wib))
    return out.transpose(1, 0, 2, 3, 4).reshape(B, T, H, d)


def dilated_attention(q, k, v):
    B, T, H, d = q.shape
    unit = Q_BLOCK * max(r for _, r in DIL_PATTERNS)
    Tp = -(-T // unit) * unit
    padw = ((0, 0), (0, Tp - T), (0, 0), (0, 0))
    qp, kp, vp = jnp.pad(q, padw), jnp.pad(k, padw), jnp.pad(v, padw)
    outs, lses = [], []
    for (w, r) in DIL_PATTERNS:
        n_sub = Tp // r
        nblk = n_sub // Q_BLOCK
        qs = qp.reshape(B, nblk, Q_BLOCK, r, H, d)
        ks = kp.reshape(B, nblk, Q_BLOCK, r, H, d)
        vs = vp.reshape(B, nblk, Q_BLOCK, r, H, d)
        kband = jnp.concatenate([jnp.concatenate([jnp.zeros_like(ks[:, :1]), ks[:, :-1]], axis=1), ks], axis=2)
        vband = jnp.concatenate([jnp.concatenate([jnp.zeros_like(vs[:, :1]), vs[:, :-1]], axis=1), vs], axis=2)
        s = jnp.einsum('bnqrhd,bnkrhd->bnrhqk', qs, kband, preferred_element_type=F32) * ATTN_SCALE
        qi = jnp.arange(Q_BLOCK)[:, None]
        kj = jnp.arange(2 * Q_BLOCK)[None, :]
        dist = qi + Q_BLOCK - kj
        band = (dist >= 0) & (dist <= w // r)
        exists = (jnp.arange(nblk)[:, None] * Q_BLOCK + jnp.arange(2 * Q_BLOCK)[None, :] - Q_BLOCK) >= 0
        mask = band[None] & exists[:, None, :]
        s = jnp.where(mask[None, :, None, None], s, -jnp.inf)
        m = jnp.max(s, axis=-1, keepdims=True)
        e = jnp.exp(s - m)
        den = jnp.sum(e, axis=-1)
        o = jnp.einsum('bnrhqk,bnkrhd->bnqrhd', e, vband.astype(F32)) / den.transpose(0, 1, 4, 2, 3)[..., None]
        lse = (m[..., 0] + jnp.log(den)).transpose(0, 1, 4, 2, 3)
        outs.append(o.reshape(B, Tp, H, d))
        lses.append(lse.reshape(B, Tp, H))
    wts = jax.nn.softmax(jnp.stack(lses, axis=0), axis=0)
    out = jnp.sum(wts[..., None] * jnp.stack(outs, axis=0), axis=0)
    return out[:, :T].astype(v.dtype)


def moba_attention(q, k, v):
    B, T, H, d = q.shape
    Tp = -(-T // MOBA_BLOCK) * MOBA_BLOCK
    padw = ((0, 0), (0, Tp - T), (0, 0), (0, 0))
    qp, kp, vp = jnp.pad(q, padw), jnp.pad(k, padw), jnp.pad(v, padw)
    nkb = Tp // MOBA_BLOCK
    topn = min(MOBA_TOPK, nkb)
    kb = kp.reshape(B, nkb, MOBA_BLOCK, H, d)
    vb = vp.reshape(B, nkb, MOBA_BLOCK, H, d)
    kmean = jnp.mean(kb.astype(F32), axis=2)
    kbt = kb.transpose(0, 3, 1, 2, 4)
    vbt = vb.transpose(0, 3, 1, 2, 4)
    nqb = Tp // MOBA_Q_BLOCK
    per = MOBA_BLOCK // MOBA_Q_BLOCK
    qbs = qp.reshape(B, nqb, MOBA_Q_BLOCK, H, d).transpose(1, 0, 2, 3, 4)
    bidx = jnp.arange(B)[:, None, None, None]
    hidx = jnp.arange(H)[None, None, :, None]
    blk_ids = jnp.arange(nkb)

    def body(args):
        i, qi = args
        own = i // per
        gate = jnp.einsum('bqhd,bnhd->bqhn', qi.astype(F32), kmean)
        gate = jnp.where(blk_ids < own, gate, -jnp.inf)
        _, sel = lax.top_k(gate, topn)
        valid = sel < own
        kg = kbt[bidx, hidx, sel]
        vg = vbt[bidx, hidx, sel]
        s_sel = jnp.einsum('bqhd,bqhnkd->bqhnk', qi, kg, preferred_element_type=F32) * ATTN_SCALE
        s_sel = jnp.where(valid[..., None], s_sel, -jnp.inf).reshape(B, MOBA_Q_BLOCK, H, topn * MOBA_BLOCK)
        kown = lax.dynamic_index_in_dim(kb, own, axis=1, keepdims=False)
        vown = lax.dynamic_index_in_dim(vb, own, axis=1, keepdims=False)
        s_own = jnp.einsum('bqhd,bkhd->bqhk', qi, kown, preferred_element_type=F32) * ATTN_SCALE
        qpos = i * MOBA_Q_BLOCK + jnp.arange(MOBA_Q_BLOCK)
        kpos = own * MOBA_BLOCK + jnp.arange(MOBA_BLOCK)
        s_own = jnp.where((kpos[None, :] <= qpos[:, None])[None, :, None, :], s_own, -jnp.inf)
        p = jax.nn.softmax(jnp.concatenate([s_sel, s_own], axis=-1), axis=-1)
        p_sel = p[..., :topn * MOBA_BLOCK].reshape(B, MOBA_Q_BLOCK, H, topn, MOBA_BLOCK)
        p_own = p[..., topn * MOBA_BLOCK:]
        o = jnp.einsum('bqhnk,bqhnkd->bqhd', p_sel.astype(v.dtype), vg) + \
            jnp.einsum('bqhk,bkhd->bqhd', p_own.astype(v.dtype), vown)
        return o.astype(v.dtype)

    out = lax.map(body, (jnp.arange(nqb), qbs))
    return out.transpose(1, 0, 2, 3, 4).reshape(B, Tp, H, d)[:, :T]


def hybrid_mixer(x, w_in, b_f, w_o):
    B, T, _ = x.shape
    pos = jnp.arange(T)
    proj = jnp.einsum('btd,de->bte', x, w_in)
    split_points = [int(s) for s in np.cumsum(SPLIT_SIZES)[:-1]]
    (qa, ka, va, fa, qb, kb, vb, qi, ki, wi, qc, kc, vc, qd, kd, vd) = jnp.split(proj, split_points, axis=-1)
    heads = lambda t: t.reshape(B, T, -1, HEAD_DIM)
    logf = jax.nn.log_sigmoid(fa.astype(F32) + b_f.astype(F32))
    oa = fox_attention(heads(qa), heads(ka), heads(va), logf)
    ob = dsa_attention(rope(heads(qb), pos), rope(heads(kb), pos), heads(vb),
                       rope(qi.reshape(B, T, IDX_HEADS, IDX_DIM), pos),
                       rope(ki[:, :, None, :], pos)[:, :, 0], wi)
    oc = dilated_attention(rope(heads(qc), pos), rope(heads(kc), pos), heads(vc))
    od = moba_attention(rope(heads(qd), pos), rope(heads(kd), pos), heads(vd))
    o = jnp.concatenate([t.reshape(B, T, -1).astype(x.dtype) for t in (oa, ob, oc, od)], axis=-1)
    return jnp.einsum('bte,ed->btd', o, w_o)


def setup_inputs(seed: int = 0) -> dict:
    key = jax.random.key(seed)
    ks = jax.random.split(key, 11)
    nrm = jax.random.normal
    x = nrm(ks[0], (BATCH, SEQ, D_MODEL), F32)
    w_in = nrm(ks[1], (DEPTH, D_MODEL, D_IN), F32) * D_MODEL ** -0.5
    b_f = 3.0 + 0.5 * nrm(ks[2], (DEPTH, H_FOX), F32)
    w_o = nrm(ks[3], (DEPTH, D_MODEL, D_MODEL), F32) * (D_MODEL ** -0.5 * DEEPNORM_BETA)
    ln1_g = 1.0 + 0.02 * nrm(ks[4], (DEPTH, D_MODEL), F32)
    ln1_b = 0.02 * nrm(ks[5], (DEPTH, D_MODEL), F32)
    w_up = nrm(ks[6], (DEPTH, D_MODEL, D_FF), F32) * D_MODEL ** -0.5
    w_down = nrm(ks[7], (DEPTH, D_FF, D_MODEL), F32) * (D_FF ** -0.5 * DEEPNORM_BETA)
    ln2_g = 1.0 + 0.02 * nrm(ks[8], (DEPTH, D_MODEL), F32)
    ln2_b = 0.02 * nrm(ks[9], (DEPTH, D_MODEL), F32)
    return {"x": x, "w_in": w_in, "b_f": b_f, "w_o": w_o, "ln1_g": ln1_g, "ln1_b": ln1_b,
            "w_up": w_up, "w_down": w_down, "ln2_g": ln2_g, "ln2_b": ln2_b}


def reference(x, w_in, b_f, w_o, ln1_g, ln1_b, w_up, w_down, ln2_g, ln2_b):
    for l in range(DEPTH):
        x = layer_norm(DEEPNORM_ALPHA * x + hybrid_mixer(x, w_in[l], b_f[l], w_o[l]), ln1_g[l], ln1_b[l])
        h = jnp.square(jax.nn.relu(jnp.einsum('btd,df->btf', x, w_up[l])))
        x = layer_norm(DEEPNORM_ALPHA * x + jnp.einsum('btf,fd->btd', h, w_down[l]), ln2_g[l], ln2_b[l])
    return x
```

```python
from contextlib import ExitStack
import numpy as np
import ml_dtypes
import concourse.bass as bass
import concourse.mybir as mybir
from concourse.bass_utils import run_bass_kernel_spmd

F32 = mybir.dt.float32
BF16 = mybir.dt.bfloat16
AF = mybir.ActivationFunctionType
ALU = mybir.AluOpType
AX = mybir.AxisListType

D = 1024
HD = 64
DFF = 4096
DEPTH = 4
SEQ = 8192
BATCH = 4
N_CORES = 8
ALPHA = (2.0 * DEPTH) ** 0.25
LN_EPS = 1e-5
NEG = -30000.0
DIN = 3372

QA0, KA0 = 0, 280
QB0, KB0 = 560, 816
QI0, KI0 = 1072, 1328
QC0, KC0 = 1360, 1616
QD0, KD0 = 1872, 2256
FM_ROWS = 2640


def _fm_chunks():
    ch = []

    def heads(col0, rope, scale, base, stride):
        for i in range(2):
            cols = list(range(col0 + i * 128, col0 + (i + 1) * 128))
            dst = [(0, 64, base + (2 * i) * stride), (64, 64, base + (2 * i + 1) * stride)]
            ch.append((cols, rope, scale, dst))

    heads(0, None, 0.125, QA0, 70)
    heads(256, None, 1.0, KA0, 70)
    ch.append((list(range(768, 772)), "fa", 1.0, []))
    heads(772, 64, 0.125, QB0, 64)
    heads(1028, 64, 1.0, KB0, 64)
    for i in range(2):
        ch.append((list(range(1540 + i * 128, 1540 + (i + 1) * 128)), 32, 1.0, [(0, 128, QI0 + i * 128)]))
    ch.append((list(range(1796, 1828)), 32, 1.0, [(0, 32, KI0)]))
    heads(1836, 64, 0.125, QC0, 64)
    heads(2092, 64, 1.0, KC0, 64)
    heads(2604, 64, 0.125, QD0, 96)
    heads(2860, 64, 1.0, KD0, 96)
    return ch


FM_CHUNKS = _fm_chunks()


def _swap_cols(cols, hd):
    out = []
    for i in range(0, len(cols), hd):
        blk = cols[i:i + hd]
        out += blk[hd // 2:] + blk[:hd // 2]
    return out


def _wfm_layout():
    cols = []
    offs = []
    for (c, rope, scale, dst) in FM_CHUNKS:
        a0 = len(cols)
        cols += c
        b0 = None
        if rope in (64, 32):
            b0 = len(cols)
            cols += _swap_cols(c, rope)
        offs.append((a0, b0, len(c)))
    return cols, offs


WFM_COLS, WFM_OFFS = _wfm_layout()
NWFM = len(WFM_COLS)
WTM_COLS = (list(range(512, 768)) + list(range(1284, 1540)) + list(range(2348, 2604)) +
            list(range(3116, 3372)) + list(range(1828, 1836)))
NWTM = len(WTM_COLS)


class Res:
    __slots__ = ("name", "w", "r")

    def __init__(self, name=""):
        self.name = name
        self.w = None
        self.r = {}


class Sched:
    def __init__(self, nc, n_dma_sems=40):
        self.nc = nc
        self.engs = {"pe": nc.tensor, "act": nc.scalar, "dve": nc.vector, "pool": nc.gpsimd, "sp": nc.sync}
        self.sem = {k: nc.alloc_semaphore(name="sem_" + k) for k in self.engs}
        self.cnt = {k: 0 for k in self.engs}
        self.seen = {k: {} for k in self.engs}
        self.dma_sems = [nc.alloc_semaphore(name=f"dsem{i}") for i in range(n_dma_sems)]
        self.dma_cnt = [0] * n_dma_sems
        self.dma_rr = 0
        self.n_inst = 0

    def _wait(self, eng, tok, war=False):
        if tok is None:
            return
        key, val = tok
        if val <= 0:
            return
        if key[0] == "e" and key[1] == eng and (eng == "pe" or eng == "sp" or war):
            return
        if self.seen[eng].get(key, 0) >= val:
            return
        h = self.sem[key[1]] if key[0] == "e" else self.dma_sems[key[1]]
        self.engs[eng].wait_ge(h, val)
        self.seen[eng][key] = val

    def _deps(self, eng, reads, writes):
        for r in reads:
            self._wait(eng, r.w)
        for w in writes:
            self._wait(eng, w.w)
            for k, v in w.r.items():
                self._wait(eng, (k, v), war=True)

    def _update(self, tok, reads, writes):
        k, v = tok
        for r in reads:
            if r.r.get(k, 0) < v:
                r.r[k] = v
        for w in writes:
            w.w = tok
            w.r = {}

    def op(self, eng, fn, reads=(), writes=(), signal=True):
        self._deps(eng, reads, writes)
        inst = fn(self.engs[eng])
        self.n_inst += 1
        if signal:
            self.cnt[eng] += 1
            inst.then_inc(self.sem[eng], 1)
            tok = (("e", eng), self.cnt[eng])
        else:
            tok = (("e", eng), self.cnt[eng] + 1)
        self._update(tok, reads, writes)
        return tok

    def dma(self, eng, out, in_, reads=(), writes=(), **kw):
        i = self.dma_rr
        self.dma_rr = (i + 1) % len(self.dma_sems)
        self._wait(eng, (("d", i), self.dma_cnt[i]))
        self._deps(eng, reads, writes)
        inst = self.engs[eng].dma_start(out=out, in_=in_, **kw)
        self.n_inst += 1
        self.dma_cnt[i] += 16
        inst.then_inc(self.dma_sems[i], 16)
        tok = (("d", i), self.dma_cnt[i])
        self._update(tok, reads, writes)
        return tok

    def barrier(self):
        toks = [(("e", k), self.cnt[k]) for k in self.engs] + \
               [(("d", i), self.dma_cnt[i]) for i in range(len(self.dma_sems))]
        for e in self.engs:
            for t in toks:
                self._wait(e, t)


def host_consts(T):
    c = {}
    pos = np.arange(T, dtype=np.float32)
    for hd, nm in ((64, "64"), (32, "32")):
        half = hd // 2
        inv = (10000.0 ** (-np.arange(half, dtype=np.float32) / half)).astype(np.float32)
        ang = pos[None, :] * inv[:, None]
        cos = np.cos(ang).astype(np.float32)
        sin = np.sin(ang).astype(np.float32)
        rows_c, rows_s = [], []
        for p in range(128):
            f = p % half
            rows_c.append(cos[f])
            rows_s.append(-sin[f] if (p % hd) < half else sin[f])
        c["cos" + nm] = np.stack(rows_c).astype(np.float32)
        c["sin" + nm] = np.stack(rows_s).astype(np.float32)
    oh = np.zeros((32, T), np.float32)
    for n in range(32):
        oh[n, n * 256:(n + 1) * 256] = 1.0
    c["onehotT"] = oh.astype(ml_dtypes.bfloat16)
    c["onesT"] = np.ones((3, T), np.float32).astype(ml_dtypes.bfloat16)
    c["ident_bf"] = np.eye(128, dtype=np.float32).astype(ml_dtypes.bfloat16)
    c["ident_f32"] = np.eye(128, dtype=np.float32)
    p = np.arange(128)[:, None]
    f = np.arange(512)[None, :]
    dm = np.zeros((128, 4, 512), np.float32)
    for m in range(4):
        dm[:, m, :] = np.where(f - p - 128 * m >= 0, 0.0, NEG)
    c["diagmask"] = dm.astype(ml_dtypes.bfloat16)
    f1 = np.arange(128)[None, :]
    dil = np.zeros((128, 2, 128), np.float32)
    dil[:, 0, :] = np.where(p >= f1, 0.0, NEG)
    dil[:, 1, :] = np.where(p <= f1, 0.0, NEG)
    c["dilmask"] = dil.astype(ml_dtypes.bfloat16)
    c["tri"] = np.where(f1 <= p, 0.0, -1e30).astype(np.float32)
    c["pow2"] = np.broadcast_to((0.5 ** np.arange(1, 20, dtype=np.float64)).astype(np.float32)[None], (128, 19)).copy()
    own = np.arange(32)[:, None]
    n = np.arange(32)[None, :]
    pb = np.where(n < own, 0.0, -1e30).astype(np.float32)
    pm = (n < own).astype(np.float32)
    om = (n == own).astype(np.float32)
    c["moba_pb"] = np.broadcast_to(np.tile(pb, (1, 4))[None], (128, 32, 128)).copy()
    c["moba_pm"] = np.broadcast_to(np.tile(pm, (1, 4))[None], (128, 32, 128)).copy()
    c["moba_om"] = np.broadcast_to(np.tile(om, (1, 4))[None], (128, 32, 128)).copy()
    return c


class Buf:
    __slots__ = ("t", "res")

    def __init__(self, t, name=""):
        self.t = t
        self.res = Res(name)


class Ring:
    def __init__(self, bufs):
        self.bufs = bufs
        self.i = 0

    def next(self):
        b = self.bufs[self.i]
        self.i = (self.i + 1) % len(self.bufs)
        return b


class Prog:
    def __init__(self, T=SEQ, depth=DEPTH, debug=False, phases=None):
        self.T = T
        self.NB = T // 128
        self.NC = T // 512
        self.depth = depth
        self.debug = debug
        self.phases = phases
        nc = bass.Bass("TRN2", target_bir_lowering=False)
        self.nc = nc
        self.S = Sched(nc)
        dk = "ExternalOutput" if debug else "Internal"
        self.x_in = nc.dram_tensor("x", [T, D], F32, kind="ExternalInput").ap()
        self.w_fm = nc.dram_tensor("w_fm", [depth, D, NWFM], F32, kind="ExternalInput").ap()
        self.w_tm = nc.dram_tensor("w_tm", [depth, D, NWTM], F32, kind="ExternalInput").ap()
        self.b_f = nc.dram_tensor("b_f", [depth, 4, 1], F32, kind="ExternalInput").ap()
        self.w_o = nc.dram_tensor("w_o", [depth, D, D], F32, kind="ExternalInput").ap()
        self.w_up = nc.dram_tensor("w_up", [depth, D, DFF], F32, kind="ExternalInput").ap()
        self.w_down = nc.dram_tensor("w_down", [depth, DFF, D], F32, kind="ExternalInput").ap()
        self.lnp = nc.dram_tensor("lnp", [depth, 4, D], F32, kind="ExternalInput").ap()
        cs = host_consts(T)
        self.cin = {}
        for k, v in cs.items():
            dt = BF16 if v.dtype == ml_dtypes.bfloat16 else F32
            self.cin[k] = nc.dram_tensor("c_" + k, list(v.shape), dt, kind="ExternalInput").ap()
        self.y = nc.dram_tensor("y", [T, D], F32, kind="ExternalOutput").ap()
        self.xres = nc.dram_tensor("xres", [T, D], F32, kind=dk).ap()
        self.xT = nc.dram_tensor("xT", [D, T], BF16, kind=dk).ap()
        self.fm = nc.dram_tensor("fm", [FM_ROWS, T], BF16, kind=dk).ap()
        self.vtm = nc.dram_tensor("vtm", [T, 1024], BF16, kind=dk).ap()
        self.oT = nc.dram_tensor("oT", [D, T], BF16, kind=dk).ap()

    def _uniq(self, name):
        self._uid = getattr(self, "_uid", 0) + 1
        return f"{name}_{self._uid}"

    def sb(self, ctx, name, shape, dt):
        return ctx.enter_context(self.nc.sbuf_tensor(self._uniq(name), shape, dt))

    def ps(self, ctx, name, shape, dt=F32):
        return ctx.enter_context(self.nc.psum_tensor(self._uniq(name), shape, dt))

    def sbring(self, ctx, name, shape, dt, n):
        return Ring([Buf(self.sb(ctx, f"{name}{i}", shape, dt), f"{name}{i}") for i in range(n)])

    def psring(self, ctx, name, shape, dt, n):
        return Ring([Buf(self.ps(ctx, f"{name}{i}", shape, dt), f"{name}{i}") for i in range(n)])

    def load_cast(self, ctx, dst, dst_slices, src_rows, ncols, name):
        S = self.S
        if not hasattr(self, "_stg") or self._stg_ctx is not ctx:
            self._stg = self.sbring(ctx, "wstage", [128, 1024], F32, 2)
            self._stg_ctx = ctx
            self._stg_flip = 0
        for c0 in range(0, ncols, 1024):
            w = min(1024, ncols - c0)
            st = self._stg.next()
            S.dma("sp", st.t[:, 0:w], src_rows[:, c0:c0 + w], writes=[st.res])
            out_ap = dst_slices(c0, w)
            if self._stg_flip % 2 == 0:
                S.op("act", lambda e: e.activation(out=out_ap, in_=st.t[:, 0:w], func=AF.Copy), reads=[st.res], writes=[dst.res])
            else:
                S.op("dve", lambda e: e.tensor_copy(out=out_ap, in_=st.t[:, 0:w]), reads=[st.res], writes=[dst.res])
            self._stg_flip += 1

    def want(self, ph):
        return self.phases is None or ph in self.phases

    def build(self):
        S = self.S
        with ExitStack() as g:
            self.ident_bf = Buf(self.sb(g, "ident_bf", [128, 128], BF16))
            self.ident_f = Buf(self.sb(g, "ident_f", [128, 128], F32))
            self.witm = Buf(self.sb(g, "witm", [128, self.NB, 8], F32))
            S.dma("sp", self.ident_bf.t[:], self.cin["ident_bf"][:, :], writes=[self.ident_bf.res])
            S.dma("sp", self.ident_f.t[:], self.cin["ident_f32"][:, :], writes=[self.ident_f.res])
            for h in range(4):
                S.dma("sp", self.fm[QA0 + h * 70 + 67:QA0 + h * 70 + 70, :], self.cin["onesT"][:, :])
                S.dma("sp", self.fm[KA0 + h * 70 + 64:KA0 + h * 70 + 67, :], self.cin["onesT"][:, :])
                S.dma("sp", self.fm[KD0 + h * 96 + 64:KD0 + h * 96 + 96, :], self.cin["onehotT"][:, :])
            S.barrier()
            if self.want("pro"):
                self.phase_prologue()
                S.barrier()
            for l in range(self.depth):
                xres_in = self.x_in if l == 0 else self.xres
                if self.want("A"):
                    self.phase_A(l)
                    S.barrier()
                    if self.debug:
                        wd = self.nc.dram_tensor(f"witm_dbg{l}", [128, self.NB, 8], F32, kind="ExternalOutput").ap()
                        S.dma("sp", wd[:, :, :], self.witm.t[:], reads=[self.witm.res])
                        S.barrier()
                if self.want("moba_prep"):
                    self.phase_moba_prep(l)
                    S.barrier()
                if self.want("fox"):
                    self.phase_dense(l, "fox")
                    S.barrier()
                if self.want("moba"):
                    self.phase_dense(l, "moba")
                    S.barrier()
                if self.want("dsa"):
                    self.phase_dsa(l)
                    S.barrier()
                if self.want("dil"):
                    self.phase_dil(l)
                    S.barrier()
                if self.want("F"):
                    self.phase_F(l, xres_in)
                    S.barrier()
                if self.want("G"):
                    self.phase_G(l, last=(l == self.depth - 1))
                    S.barrier()
        return self.nc

    def emit_xT(self, tb, src, tp_ring, st_ring):
        S = self.S
        tp = tp_ring.next()
        for dc in range(8):
            S.op("pe", lambda e, dc=dc: e.transpose(out=tp.t[:, dc, :], in_=src.t[:, dc * 128:(dc + 1) * 128],
                                                   identity=self.ident_bf.t[:]),
                 reads=[src.res, self.ident_bf.res], writes=[tp.res], signal=(dc == 7))
        st = st_ring.next()
        S.op("act", lambda e: e.activation(out=st.t[:], in_=tp.t[:], func=AF.Copy),
             reads=[tp.res], writes=[st.res])
        dst = self.xT.rearrange("(dc p) t -> p dc t", p=128)[:, :, tb * 128:(tb + 1) * 128]
        S.dma("sp", dst, st.t[:], reads=[st.res])

    def phase_prologue(self):
        S = self.S
        with ExitStack() as c:
            xin = self.sbring(c, "pro_x", [128, D], F32, 2)
            xbf = self.sbring(c, "pro_xb", [128, D], BF16, 2)
            tp = self.psring(c, "pro_tp", [128, 8, 128], BF16, 2)
            st = self.sbring(c, "pro_st", [128, 8, 128], BF16, 2)
            for tb in range(self.NB):
                a = xin.next()
                S.dma("sp", a.t[:], self.x_in[tb * 128:(tb + 1) * 128, :], writes=[a.res])
                b = xbf.next()
                S.op("dve", lambda e: e.tensor_copy(out=b.t[:], in_=a.t[:]), reads=[a.res], writes=[b.res])
                self.emit_xT(tb, b, tp, st)

    def phase_A(self, l):
        S = self.S
        nc = self.nc
        T = self.T
        with ExitStack() as c:
            wfm = Buf(self.sb(c, "A_wfm", [128, 8, NWFM], BF16))
            wtm = Buf(self.sb(c, "A_wtm", [128, 8, NWTM], BF16))
            nbf = Buf(self.sb(c, "A_nbf", [4, 1], F32))
            bfr = Buf(self.sb(c, "A_bf", [4, 1], F32))
            for dc in range(8):
                self.load_cast(c, wfm, lambda c0, w, dc=dc: wfm.t[:, dc, c0:c0 + w], self.w_fm[l, dc * 128:(dc + 1) * 128, :], NWFM, "A")
                self.load_cast(c, wtm, lambda c0, w, dc=dc: wtm.t[:, dc, c0:c0 + w], self.w_tm[l, dc * 128:(dc + 1) * 128, :], NWTM, "A")
            S.dma("sp", bfr.t[:], self.b_f[l], writes=[bfr.res])
            S.op("dve", lambda e: e.tensor_scalar(out=nbf.t[:], in0=bfr.t[:], scalar1=-1.0, scalar2=None, op0=ALU.mult),
                 reads=[bfr.res], writes=[nbf.res])
            xTc = self.sbring(c, "A_xT", [128, 8, 512], BF16, 2)
            tabs = {k: self.sbring(c, "A_" + k, [128, 512], F32, 2) for k in ("cos64", "sin64", "cos32", "sin32")}
            psA = self.psring(c, "A_psA", [128, 512], F32, 2)
            psB = self.psring(c, "A_psB", [128, 512], F32, 2)
            psV = self.psring(c, "A_psV", [128, 1024], F32, 1)
            psW = self.psring(c, "A_psW", [128, 8], F32, 1)
            t1r = self.sbring(c, "A_t1", [128, 512], F32, 2)
            t2r = self.sbring(c, "A_t2", [128, 512], F32, 2)
            stg = self.sbring(c, "A_stg", [128, 512], BF16, 3)
            vst = self.sbring(c, "A_vst", [128, 1024], BF16, 2)
            e_t = self.sbring(c, "A_e", [4, 512], F32, 2)
            ones4 = Buf(self.sb(c, "A_ones4", [4, 512], F32))
            c_t = self.sbring(c, "A_c", [4, 512], F32, 2)
            r_t = self.sbring(c, "A_r", [4, 512], F32, 2)
            p6 = self.sbring(c, "A_p6", [4, 6, 512], BF16, 2)
            n6 = self.sbring(c, "A_n6", [4, 6, 512], BF16, 2)
            S.op("dve", lambda e: e.memset(ones4.t[:], 1.0), writes=[ones4.res])
            c_prev = None
            for ck in range(self.NC):
                cs = slice(ck * 512, (ck + 1) * 512)
                xt = xTc.next()
                S.dma("sp", xt.t[:], self.xT.rearrange("(dc p) t -> p dc t", p=128)[:, :, cs], writes=[xt.res])
                tb_ = {}
                for k in tabs:
                    tb_[k] = tabs[k].next()
                    S.dma("sp", tb_[k].t[:], self.cin[k][:, cs], writes=[tb_[k].res])
                for gi, (cols, rope, scale, dst) in enumerate(FM_CHUNKS):
                    a0, b0, n = WFM_OFFS[gi]
                    pa = psA.next()
                    for dc in range(8):
                        S.op("pe", lambda e, dc=dc: e.matmul(pa.t[0:n, :], lhsT=wfm.t[:, dc, a0:a0 + n], rhs=xt.t[:, dc, :],
                                                             start=(dc == 0), stop=(dc == 7)),
                             reads=[wfm.res, xt.res], writes=[pa.res], signal=(dc == 7))
                    if rope == "fa":
                        e1 = e_t.next()
                        S.op("act", lambda e: e.activation(out=e1.t[:], in_=pa.t[0:4, :], func=AF.Exp, bias=nbf.t[:], scale=-1.0),
                             reads=[pa.res, nbf.res], writes=[e1.res])
                        S.op("act", lambda e: e.activation(out=e1.t[:], in_=e1.t[:], func=AF.Ln, bias=1.0, scale=1.0),
                             reads=[e1.res], writes=[e1.res])
                        cc = c_t.next()
                        init = 0.0 if c_prev is None else c_prev.t[:, 511:512]
                        rd = [ones4.res, e1.res] + ([] if c_prev is None else [c_prev.res])
                        S.op("dve", lambda e: e.tensor_tensor_scan(out=cc.t[:], data0=ones4.t[:], data1=e1.t[:], initial=init,
                                                                   op0=ALU.mult, op1=ALU.subtract),
                             reads=rd, writes=[cc.res])
                        c_prev = cc
                        P = p6.next()
                        N6 = n6.next()
                        R = r_t.next()
                        S.op("dve", lambda e: e.tensor_copy(out=P.t[:, 0, :], in_=cc.t[:]), reads=[cc.res], writes=[P.res])
                        S.op("dve", lambda e: e.tensor_tensor(out=R.t[:], in0=cc.t[:], in1=P.t[:, 0, :], op=ALU.subtract),
                             reads=[cc.res, P.res], writes=[R.res])
                        S.op("dve", lambda e: e.tensor_copy(out=P.t[:, 1, :], in_=R.t[:]), reads=[R.res], writes=[P.res])
                        S.op("dve", lambda e: e.tensor_tensor(out=R.t[:], in0=R.t[:], in1=P.t[:, 1, :], op=ALU.subtract),
                             reads=[R.res, P.res], writes=[R.res])
                        S.op("dve", lambda e: e.tensor_copy(out=P.t[:, 2, :], in_=R.t[:]), reads=[R.res], writes=[P.res])
                        S.op("dve", lambda e: e.tensor_scalar(out=N6.t[:, 0:3, :], in0=P.t[:, 0:3, :], scalar1=-1.0, scalar2=None,
                                                              op0=ALU.mult), reads=[P.res], writes=[N6.res])
                        qdst = self.fm[QA0:QA0 + 280, cs].rearrange("(h r) t -> h r t", r=70)[:, 64:67, :]
                        kdst = self.fm[KA0:KA0 + 280, cs].rearrange("(h r) t -> h r t", r=70)[:, 67:70, :]
                        S.dma("sp", qdst, P.t[:, 0:3, :], reads=[P.res])
                        S.dma("sp", kdst, N6.t[:, 0:3, :], reads=[N6.res])
                        continue
                    sg = stg.next()
                    if rope is None:
                        S.op("act", lambda e: e.activation(out=sg.t[0:n, :], in_=pa.t[0:n, :], func=AF.Copy, scale=float(scale)),
                             reads=[pa.res], writes=[sg.res])
                    else:
                        pb = psB.next()
                        for dc in range(8):
                            S.op("pe", lambda e, dc=dc: e.matmul(pb.t[0:n, :], lhsT=wfm.t[:, dc, b0:b0 + n], rhs=xt.t[:, dc, :],
                                                                 start=(dc == 0), stop=(dc == 7)),
                                 reads=[wfm.res, xt.res], writes=[pb.res], signal=(dc == 7))
                        ct = tb_["cos%d" % rope]
                        st_ = tb_["sin%d" % rope]
                        t1 = t1r.next()
                        t2 = t2r.next()
                        S.op("dve", lambda e: e.scalar_tensor_tensor(out=t1.t[0:n, :], in0=pa.t[0:n, :], scalar=float(scale),
                                                                     in1=ct.t[0:n, :], op0=ALU.mult, op1=ALU.mult),
                             reads=[pa.res, ct.res], writes=[t1.res])
                        S.op("dve", lambda e: e.scalar_tensor_tensor(out=t2.t[0:n, :], in0=pb.t[0:n, :], scalar=float(scale),
                                                                     in1=st_.t[0:n, :], op0=ALU.mult, op1=ALU.mult),
                             reads=[pb.res, st_.res], writes=[t2.res])
                        S.op("dve", lambda e: e.tensor_tensor(out=sg.t[0:n, :], in0=t1.t[0:n, :], in1=t2.t[0:n, :], op=ALU.add),
                             reads=[t1.res, t2.res], writes=[sg.res])
                    for (ro, nr, frow) in dst:
                        S.dma("sp", self.fm[frow:frow + nr, cs], sg.t[ro:ro + nr, :], reads=[sg.res])
                for j in range(4):
                    tb = ck * 4 + j
                    ts = slice(j * 128, (j + 1) * 128)
                    pv = psV.next()
                    pw = psW.next()
                    for half in range(2):
                        for dc in range(8):
                            S.op("pe", lambda e, dc=dc, half=half: e.matmul(pv.t[:, half * 512:(half + 1) * 512],
                                                                            lhsT=xt.t[:, dc, ts],
                                                                            rhs=wtm.t[:, dc, half * 512:(half + 1) * 512],
                                                                            start=(dc == 0), stop=(dc == 7)),
                                 reads=[wtm.res, xt.res], writes=[pv.res], signal=(dc == 7))
                    for dc in range(8):
                        S.op("pe", lambda e, dc=dc: e.matmul(pw.t[:, :], lhsT=xt.t[:, dc, ts], rhs=wtm.t[:, dc, 1024:1032],
                                                             start=(dc == 0), stop=(dc == 7)),
                             reads=[wtm.res, xt.res], writes=[pw.res], signal=(dc == 7))
                    vs = vst.next()
                    S.op("act", lambda e: e.activation(out=vs.t[:], in_=pv.t[:], func=AF.Copy), reads=[pv.res], writes=[vs.res])
                    S.dma("sp", self.vtm[tb * 128:(tb + 1) * 128, :], vs.t[:], reads=[vs.res])
                    S.op("dve", lambda e, tb=tb: e.tensor_copy(out=self.witm.t[:, tb, :], in_=pw.t[:, :]),
                         reads=[pw.res], writes=[self.witm.res])


def prep_shared(T, depth, w_in, b_f, w_o, ln1_g, ln1_b, w_up, w_down, ln2_g, ln2_b):
    m = {}
    w_in = np.asarray(w_in, np.float32)
    m["w_fm"] = np.ascontiguousarray(w_in[:depth][:, :, WFM_COLS])
    m["w_tm"] = np.ascontiguousarray(w_in[:depth][:, :, WTM_COLS])
    m["b_f"] = np.ascontiguousarray(np.asarray(b_f, np.float32)[:depth].reshape(depth, 4, 1))
    m["w_o"] = np.ascontiguousarray(np.asarray(w_o, np.float32)[:depth])
    m["w_up"] = np.ascontiguousarray(np.asarray(w_up, np.float32)[:depth])
    m["w_down"] = np.ascontiguousarray(np.asarray(w_down, np.float32)[:depth])
    m["lnp"] = np.ascontiguousarray(np.stack([np.asarray(a, np.float32)[:depth] for a in (ln1_g, ln1_b, ln2_g, ln2_b)], axis=1))
    for k, v in host_consts(T).items():
        m["c_" + k] = v
    return m


def _finalize(P, S, po, n, rrow, ones_row, psB, bcs_ring, ost_ring, dst):
    S.op("dve", lambda e: e.reciprocal(out=rrow.t[64:65, 0:n], in_=po.t[64:65, 0:n]), reads=[po.res], writes=[rrow.res])
    pb = psB.next()
    S.op("pe", lambda e: e.matmul(pb.t[0:64, 0:n], lhsT=ones_row.t[64:65, 0:64], rhs=rrow.t[64:65, 0:n], start=True, stop=True),
         reads=[ones_row.res, rrow.res], writes=[pb.res])
    bc = bcs_ring.next()
    S.op("act", lambda e: e.activation(out=bc.t[0:64, 0:n], in_=pb.t[0:64, 0:n], func=AF.Copy), reads=[pb.res], writes=[bc.res])
    os_ = ost_ring.next()
    S.op("dve", lambda e: e.tensor_tensor(out=os_.t[0:64, 0:n], in0=po.t[0:64, 0:n], in1=bc.t[0:64, 0:n], op=ALU.mult),
         reads=[po.res, bc.res], writes=[os_.res])
    S.dma("sp", dst, os_.t[0:64, 0:n], reads=[os_.res])


def phase_dense(self, l, kind):
    S = self.S
    T, NB, NCK = self.T, self.NB, self.NC
    KR = 70 if kind == "fox" else 96
    q0, k0 = (QA0, KA0) if kind == "fox" else (QD0, KD0)
    g = 0 if kind == "fox" else 3
    with ExitStack() as c:
        Qp = self.sbring(c, "dn_Q", [96, T], BF16, 2)
        Kp = self.sbring(c, "dn_K", [96, T], BF16, 2)
        Vp = self.sbring(c, "dn_V", [128, NB, 65], BF16, 2)
        dmask = Buf(self.sb(c, "dn_mask", [128, 4, 512], BF16))
        ones_row = Buf(self.sb(c, "dn_ones", [128, 64], F32))
        rrow = Buf(self.sb(c, "dn_rrow", [128, 512], F32))
        bcs = self.sbring(c, "dn_bcs", [64, 512], F32, 2)
        ost = self.sbring(c, "dn_ost", [64, 512], BF16, 2)
        pt = self.sbring(c, "dn_pt", [128, 512], BF16, 5)
        psS = self.psring(c, "dn_psS", [128, 512], F32, 4)
        psO = self.psring(c, "dn_psO", [128, 512], F32, 2)
        psB = self.psring(c, "dn_psB", [128, 512], F32, 1)
        S.dma("sp", dmask.t[:], self.cin["diagmask"][:, :, :], writes=[dmask.res])
        S.op("dve", lambda e: e.memset(ones_row.t[:], 1.0), writes=[ones_row.res])
        for b in Vp.bufs:
            S.op("pool", lambda e, b=b: e.memset(b.t[:, :, 64:65], 1.0), writes=[b.res])
        for h in range(4):
            Q, K_, V = Qp.next(), Kp.next(), Vp.next()
            S.dma("sp", Q.t[0:KR, :], self.fm[q0 + h * KR:q0 + (h + 1) * KR, :], writes=[Q.res])
            S.dma("sp", K_.t[0:KR, :], self.fm[k0 + h * KR:k0 + (h + 1) * KR, :], writes=[K_.res])
            col = g * 256 + h * 64
            S.dma("sp", V.t[:, :, 0:64], self.vtm.rearrange("(nb p) c -> p nb c", p=128)[:, :, col:col + 64], writes=[V.res])
            items = [(qc, j) for qc in range(NCK) for j in range(4 * qc + 4)]
            LA = 3
            pos = {}
            pq = {}

            def stage1(it, qc, j):
                qs = slice(qc * 512, (qc + 1) * 512)
                if j == 0:
                    pos[qc] = psO.next()
                ps = psS.next()
                m = j - 4 * qc
                S.op("pe", lambda e: e.matmul(ps.t[:, :], lhsT=K_.t[0:KR, j * 128:(j + 1) * 128], rhs=Q.t[0:KR, qs],
                                              start=True, stop=(m < 0)),
                     reads=[K_.res, Q.res], writes=[ps.res], signal=(m < 0))
                if m >= 0:
                    S.op("pe", lambda e: e.matmul(ps.t[:, :], lhsT=self.ident_bf.t[:], rhs=dmask.t[:, m, :], start=False, stop=True),
                         reads=[self.ident_bf.res, dmask.res], writes=[ps.res])
                p = pt.next()
                S.op("act", lambda e: e.activation(out=p.t[:], in_=ps.t[:], func=AF.Exp), reads=[ps.res], writes=[p.res])
                pq[it] = p

            def stage2(it, qc, j):
                nj = 4 * qc + 4
                po = pos[qc]
                p = pq.pop(it)
                S.op("pe", lambda e: e.matmul(po.t[0:65, :], lhsT=V.t[:, j, 0:65], rhs=p.t[:], start=(j == 0), stop=(j == nj - 1)),
                     reads=[V.res, p.res], writes=[po.res], signal=(j == nj - 1))
                if j == nj - 1:
                    row = (g * 4 + h) * 64
                    _finalize(self, S, po, 512, rrow, ones_row, psB, bcs, ost, self.oT[row:row + 64, qc * 512:(qc + 1) * 512])

            for it in range(len(items) + LA):
                if it < len(items):
                    stage1(it, *items[it])
                if it - LA >= 0:
                    stage2(it - LA, *items[it - LA])


Prog.phase_dense = phase_dense


def phase_moba_prep(self, l):
    S = self.S
    T, NB = self.T, self.NB
    nkb = T // 256
    with ExitStack() as c:
        Kd = [Buf(self.sb(c, f"mp_K{i}", [64, T], BF16)) for i in range(4)]
        Qd = [Buf(self.sb(c, f"mp_Q{i}", [64, T], BF16)) for i in range(4)]
        kmf = [Buf(self.sb(c, f"mp_kmf{i}", [64, 32], F32)) for i in range(4)]
        kmb = [Buf(self.sb(c, f"mp_kmb{i}", [64, 32], BF16)) for i in range(4)]
        tabs = {k: Buf(self.sb(c, "mp_" + k, [128, 32, 128], F32)) for k in ("moba_pb", "moba_pm", "moba_om")}
        for k in tabs:
            S.dma("sp", tabs[k].t[:], self.cin[k][:, :, :], writes=[tabs[k].res])
        gt = self.sbring(c, "mp_g", [128, 128], F32, 2)
        m8 = self.sbring(c, "mp_m8", [128, 4, 8], F32, 2)
        At = self.sbring(c, "mp_A", [128, 128], F32, 2)
        mbf = self.sbring(c, "mp_mbf", [128, 128], F32, 2)
        stg = self.sbring(c, "mp_stg", [128, 512], BF16, 2)
        psG = self.psring(c, "mp_psG", [128, 128], F32, 2)
        psT = self.psring(c, "mp_psT", [128, 512], F32, 2)
        for h in range(4):
            S.dma("sp", Kd[h].t[:, :], self.fm[KD0 + h * 96:KD0 + h * 96 + 64, :], writes=[Kd[h].res])
            S.dma("sp", Qd[h].t[:, :], self.fm[QD0 + h * 96:QD0 + h * 96 + 64, :], writes=[Qd[h].res])
            S.op("dve", lambda e, h=h: e.memset(kmf[h].t[:], 0.0), writes=[kmf[h].res])
            S.op("dve", lambda e, h=h: e.tensor_reduce(out=kmf[h].t[:, 0:nkb], in_=Kd[h].t[:].rearrange("p (n k) -> p n k", k=256),
                                                      axis=AX.X, op=ALU.add), reads=[Kd[h].res], writes=[kmf[h].res])
            S.op("dve", lambda e, h=h: e.tensor_scalar(out=kmb[h].t[:], in0=kmf[h].t[:], scalar1=1.0 / 256, scalar2=None, op0=ALU.mult),
                 reads=[kmf[h].res], writes=[kmb[h].res])
        for tb4 in range(NB // 4):
            pT = psT.next()
            for j in range(4):
                tb = tb4 * 4 + j
                own = tb // 2
                ts = slice(tb * 128, (tb + 1) * 128)
                pg = psG.next()
                for h in range(4):
                    S.op("pe", lambda e, h=h: e.matmul(pg.t[:, h * 32:(h + 1) * 32], lhsT=Qd[h].t[:, ts],
                                                       rhs=kmb[h].t[:, :], start=True, stop=True),
                         reads=[Qd[h].res, kmb[h].res], writes=[pg.res], signal=(h == 3))
                gg = gt.next()
                S.op("dve", lambda e: e.tensor_tensor(out=gg.t[:], in0=pg.t[:], in1=tabs["moba_pb"].t[:, own, :], op=ALU.add),
                     reads=[pg.res, tabs["moba_pb"].res], writes=[gg.res])
                mm = m8.next()
                for h in range(4):
                    S.op("dve", lambda e, h=h: e.max(out=mm.t[:, h, :], in_=gg.t[:, h * 32:(h + 1) * 32]), reads=[gg.res], writes=[mm.res])
                A = At.next()
                for h in range(4):
                    S.op("dve", lambda e, h=h: e.tensor_scalar(out=A.t[:, h * 32:(h + 1) * 32], in0=gg.t[:, h * 32:(h + 1) * 32],
                                                               scalar1=mm.t[:, h, 2:3], scalar2=None, op0=ALU.is_ge),
                         reads=[gg.res, mm.res], writes=[A.res])
                S.op("dve", lambda e: e.tensor_tensor(out=A.t[:], in0=A.t[:], in1=tabs["moba_pm"].t[:, own, :], op=ALU.mult),
                     reads=[A.res, tabs["moba_pm"].res], writes=[A.res])
                S.op("dve", lambda e: e.tensor_tensor(out=A.t[:], in0=A.t[:], in1=tabs["moba_om"].t[:, own, :], op=ALU.add),
                     reads=[A.res, tabs["moba_om"].res], writes=[A.res])
                mb = mbf.next()
                S.op("dve", lambda e: e.tensor_scalar(out=mb.t[:], in0=A.t[:], scalar1=-1.0, scalar2=-NEG, op0=ALU.add, op1=ALU.mult),
                     reads=[A.res], writes=[mb.res])
                S.op("pe", lambda e, j=j: e.transpose(out=pT.t[:, j * 128:(j + 1) * 128], in_=mb.t[:], identity=self.ident_f.t[:]),
                     reads=[mb.res, self.ident_f.res], writes=[pT.res])
            sg = stg.next()
            S.op("act", lambda e: e.activation(out=sg.t[:], in_=pT.t[:], func=AF.Copy), reads=[pT.res], writes=[sg.res])
            for h in range(4):
                S.dma("sp", self.fm[QD0 + h * 96 + 64:QD0 + h * 96 + 96, tb4 * 512:(tb4 + 1) * 512], sg.t[h * 32:(h + 1) * 32, :], reads=[sg.res])


Prog.phase_moba_prep = phase_moba_prep


NIT = 13
U8 = mybir.dt.uint8
NDVE = 8


def phase_dsa(self, l):
    S = self.S
    T, NB = self.T, self.NB
    with ExitStack() as c:
        ki = Buf(self.sb(c, "ds_ki", [32, T], BF16))
        Kb = [Buf(self.sb(c, f"ds_K{i}", [128, T], BF16)) for i in range(2)]
        Vb = Buf(self.sb(c, "ds_V", [128, NB, 4, 65], BF16))
        accr = self.sbring(c, "ds_acc", [128, T], F32, 2)
        mb = Buf(self.sb(c, "ds_mb", [128, T], BF16))
        junk = Buf(self.sb(c, "ds_junk", [128, T], U8))
        tri = Buf(self.sb(c, "ds_tri", [128, 128], F32))
        powt = Buf(self.sb(c, "ds_pow", [128, NIT + 1], F32))
        ones_row = Buf(self.sb(c, "ds_ones", [128, 64], F32))
        rrow = Buf(self.sb(c, "ds_rrow", [128, 512], F32))
        qi_r = self.sbring(c, "ds_qi", [32, 8, 128], BF16, 2)
        qb_r = [self.sbring(c, f"ds_qb{h}", [128, 128], BF16, 3) for h in range(4)]
        dg_r = self.sbring(c, "ds_dg", [128, 4, 128], F32, 2)
        rr = self.sbring(c, "ds_r", [128, 512], F32, 4)
        pt = self.sbring(c, "ds_pt", [128, 512], BF16, 3)
        bcs = self.sbring(c, "ds_bcs", [64, 512], F32, 1)
        ost = self.sbring(c, "ds_ost", [64, 512], BF16, 2)
        sm = self.sbring(c, "ds_sm", [128, 16], F32, 2)
        wt = self.sbring(c, "ds_wt", [128, NIT + 1], F32, 2)
        m8 = self.sbring(c, "ds_m8", [128, 8], F32, 2)
        psX = self.psring(c, "ds_psX", [128, 512], F32, 2)
        psA = self.psring(c, "ds_psA", [128, 512], F32, 2)
        psS = self.psring(c, "ds_psS", [128, 512], F32, 2)
        psO = self.psring(c, "ds_psO", [128, 512], F32, 1)
        psB = self.psring(c, "ds_psB", [128, 512], F32, 1)
        S.dma("sp", ki.t[:, :], self.fm[KI0:KI0 + 32, :], writes=[ki.res])
        for i in range(2):
            S.dma("sp", Kb[i].t[:, :], self.fm[KB0 + i * 128:KB0 + (i + 1) * 128, :], writes=[Kb[i].res])
        for h in range(4):
            for b in qb_r[h].bufs:
                S.op("pool", lambda e, b=b: e.memset(b.t[:], 0.0), writes=[b.res])
        for h in range(4):
            S.dma("sp", Vb.t[:, :, h, 0:64], self.vtm.rearrange("(nb p) c -> p nb c", p=128)[:, :, 256 + h * 64:320 + h * 64],
                  writes=[Vb.res])
        S.op("pool", lambda e: e.memset(Vb.t[:, :, :, 64:65], 1.0), writes=[Vb.res])
        S.dma("sp", tri.t[:], self.cin["tri"][:, :], writes=[tri.res])
        S.dma("sp", powt.t[:], self.cin["pow2"][:, 0:NIT + 1], writes=[powt.res])
        S.op("dve", lambda e: e.memset(ones_row.t[:], 1.0), writes=[ones_row.res])
        st = {}

        def index_scores(tb):
            Sk = (tb + 1) * 128
            ts = slice(tb * 128, (tb + 1) * 128)
            qi = qi_r.next()
            S.dma("sp", qi.t[:], self.fm[QI0:QI0 + 256, ts].rearrange("(h e) t -> e h t", e=32), writes=[qi.res])
            qb = [qb_r[h].next() for h in range(4)]
            for h in range(4):
                hr = (h % 2) * 64
                S.dma("sp", qb[h].t[hr:hr + 64, :], self.fm[QB0 + h * 64:QB0 + (h + 1) * 64, ts], writes=[qb[h].res])
            st[tb] = qb
            acc = accr.next()
            st[(tb, "acc")] = acc
            dg = dg_r.next()
            for h in range(max(NDVE, 4), 8):
                S.op("pool", lambda e, h=h: e.tensor_scalar(out=dg.t[:, h - 4, :], in0=self.ident_f.t[:], scalar1=self.witm.t[:, tb, h:h + 1],
                                                            scalar2=None, op0=ALU.mult),
                     reads=[self.ident_f.res, self.witm.res], writes=[dg.res])
            items = [(kc, h) for kc in range((Sk + 511) // 512) for h in range(8)]
            pend = {}
            accp = {}

            def s1(it, kc, h):
                w = min(512, Sk - kc * 512)
                px = psX.next()
                S.op("pe", lambda e: e.matmul(px.t[:, 0:w], lhsT=qi.t[:, h, :], rhs=ki.t[:, kc * 512:kc * 512 + w], start=True, stop=True),
                     reads=[qi.res, ki.res], writes=[px.res])
                r = rr.next()
                S.op("act", lambda e: e.activation(out=r.t[:, 0:w], in_=px.t[:, 0:w], func=AF.Relu), reads=[px.res], writes=[r.res])
                pend[it] = r

            def s2(it, kc, h):
                w = min(512, Sk - kc * 512)
                ks = slice(kc * 512, kc * 512 + w)
                r = pend.pop(it)
                if h == 0:
                    S.op("dve", lambda e: e.tensor_scalar(out=acc.t[:, ks], in0=r.t[:, 0:w], scalar1=self.witm.t[:, tb, 0:1],
                                                          scalar2=None, op0=ALU.mult),
                         reads=[r.res, self.witm.res], writes=[acc.res])
                elif h < NDVE:
                    S.op("dve", lambda e: e.scalar_tensor_tensor(out=acc.t[:, ks], in0=r.t[:, 0:w], scalar=self.witm.t[:, tb, h:h + 1],
                                                                 in1=acc.t[:, ks], op0=ALU.mult, op1=ALU.add),
                         reads=[r.res, self.witm.res, acc.res], writes=[acc.res])
                else:
                    if h == NDVE:
                        accp[kc] = psA.next()
                    pa = accp[kc]
                    S.op("pe", lambda e: e.matmul(pa.t[:, 0:w], lhsT=dg.t[:, h - 4, :], rhs=r.t[:, 0:w], start=(h == NDVE), stop=(h == 7)),
                         reads=[dg.res, r.res], writes=[pa.res], signal=(h == 7))
                    if h == 7:
                        S.op("dve", lambda e: e.tensor_tensor(out=acc.t[:, ks], in0=pa.t[:, 0:w], in1=acc.t[:, ks], op=ALU.add),
                             reads=[pa.res, acc.res], writes=[acc.res])

            for it in range(len(items) + 1):
                if it < len(items):
                    s1(it, *items[it])
                if it >= 1:
                    s2(it - 1, *items[it - 1])
                    yield

        def threshold_mask(tb):
            Sk = (tb + 1) * 128
            ts = slice(tb * 128, (tb + 1) * 128)
            s = sm.next()
            acc = st.pop((tb, "acc"))
            if tb >= 2:
                mx = m8.next()
                S.op("dve", lambda e: e.max(out=mx.t[:], in_=acc.t[:, 0:Sk]), reads=[acc.res], writes=[mx.res])
                S.op("dve", lambda e: e.tensor_reduce(out=s.t[:, 0:1], in_=acc.t[:, 0:Sk], axis=AX.X, op=ALU.min),
                     reads=[acc.res], writes=[s.res])
                S.op("dve", lambda e: e.tensor_tensor(out=s.t[:, 1:2], in0=mx.t[:, 0:1], in1=s.t[:, 0:1], op=ALU.subtract),
                     reads=[mx.res, s.res], writes=[s.res])
                S.op("dve", lambda e: e.tensor_scalar(out=s.t[:, 1:2], in0=s.t[:, 1:2], scalar1=1.0001, scalar2=1e-3, op0=ALU.mult, op1=ALU.add),
                     reads=[s.res], writes=[s.res])
                wtt = wt.next()
                S.op("dve", lambda e: e.tensor_scalar(out=wtt.t[:], in0=powt.t[:], scalar1=s.t[:, 1:2], scalar2=None, op0=ALU.mult),
                     reads=[powt.res, s.res], writes=[wtt.res])
                S.op("dve", lambda e: e.tensor_tensor(out=s.t[:, 2:3], in0=s.t[:, 0:1], in1=wtt.t[:, 0:1], op=ALU.add),
                     reads=[s.res, wtt.res], writes=[s.res])
            S.op("dve", lambda e: e.tensor_tensor(out=acc.t[:, ts], in0=acc.t[:, ts], in1=tri.t[:], op=ALU.add),
                 reads=[acc.res, tri.res], writes=[acc.res])
            if tb >= 2:
                for it in range(NIT):
                    S.op("dve", lambda e: e.tensor_scalar(out=junk.t[:, 0:Sk], in0=acc.t[:, 0:Sk], scalar1=s.t[:, 2:3], scalar2=None,
                                                          op0=ALU.is_ge, op1=ALU.add, accum_out=s.t[:, 3:4]),
                         reads=[acc.res, s.res], writes=[junk.res, s.res])
                    yield
                    S.op("dve", lambda e, it=it: e.scalar_tensor_tensor(out=s.t[:, 4:5], in0=s.t[:, 3:4], scalar=255.5, in1=wtt.t[:, it:it + 1],
                                                                        op0=ALU.is_ge, op1=ALU.mult),
                         reads=[s.res, wtt.res], writes=[s.res])
                    S.op("dve", lambda e, it=it: e.scalar_tensor_tensor(out=s.t[:, 2:3], in0=s.t[:, 2:3], scalar=wtt.t[:, it + 1:it + 2],
                                                                        in1=s.t[:, 4:5], op0=ALU.subtract, op1=ALU.add),
                         reads=[s.res, wtt.res], writes=[s.res])
                S.op("dve", lambda e: e.tensor_tensor(out=s.t[:, 5:6], in0=s.t[:, 2:3], in1=wtt.t[:, NIT:NIT + 1], op=ALU.subtract),
                     reads=[s.res, wtt.res], writes=[s.res])
            else:
                S.op("dve", lambda e: e.memset(s.t[:, 5:6], -1e29), writes=[s.res])
            yield "final"
            S.op("dve", lambda e: e.tensor_scalar(out=mb.t[:, 0:Sk], in0=acc.t[:, 0:Sk], scalar1=s.t[:, 5:6], scalar2=NEG,
                                                  op0=ALU.is_lt, op1=ALU.mult),
                 reads=[acc.res, s.res], writes=[mb.res])

        def attention(tb):
            qb = st[tb]
            po = psO.next()
            st[(tb, "po")] = po
            items = [(h, gi) for h in range(4) for gi in range((tb + 4) // 4)]
            pend = {}

            def s1(it, h, gi):
                j0 = gi * 4
                nb = min(4, tb + 1 - j0)
                ps = psS.next()
                for jj in range(nb):
                    j = j0 + jj
                    S.op("pe", lambda e, j=j, jj=jj: e.matmul(ps.t[:, jj * 128:(jj + 1) * 128], lhsT=Kb[h // 2].t[:, j * 128:(j + 1) * 128],
                                                              rhs=qb[h].t[:, :], start=True, stop=False),
                         reads=[Kb[h // 2].res, qb[h].res], writes=[ps.res], signal=False)
                    S.op("pe", lambda e, j=j, jj=jj: e.matmul(ps.t[:, jj * 128:(jj + 1) * 128], lhsT=mb.t[:, j * 128:(j + 1) * 128],
                                                              rhs=self.ident_bf.t[:], start=False, stop=True),
                         reads=[mb.res, self.ident_bf.res], writes=[ps.res], signal=(jj == nb - 1))
                p = pt.next()
                S.op("act", lambda e: e.activation(out=p.t[:, 0:nb * 128], in_=ps.t[:, 0:nb * 128], func=AF.Exp),
                     reads=[ps.res], writes=[p.res])
                pend[it] = p

            def s2(it, h, gi):
                j0 = gi * 4
                nb = min(4, tb + 1 - j0)
                p = pend.pop(it)
                for jj in range(nb):
                    j = j0 + jj
                    last = (j == tb)
                    S.op("pe", lambda e, j=j, jj=jj, last=last: e.matmul(po.t[0:65, h * 128:(h + 1) * 128], lhsT=Vb.t[:, j, h, :],
                                                                          rhs=p.t[:, jj * 128:(jj + 1) * 128], start=(j == 0), stop=last),
                         reads=[Vb.res, p.res], writes=[po.res], signal=last)

            for it in range(len(items) + 1):
                if it < len(items):
                    s1(it, *items[it])
                if it >= 1:
                    s2(it - 1, *items[it - 1])
                    yield

        def finalize(tb):
            po = st.pop((tb, "po"))
            st.pop(tb)
            ts = slice(tb * 128, (tb + 1) * 128)
            S.op("dve", lambda e: e.reciprocal(out=rrow.t[64:65, :], in_=po.t[64:65, :]), reads=[po.res], writes=[rrow.res])
            pb = psB.next()
            S.op("pe", lambda e: e.matmul(pb.t[0:64, :], lhsT=ones_row.t[64:65, 0:64], rhs=rrow.t[64:65, :], start=True, stop=True),
                 reads=[ones_row.res, rrow.res], writes=[pb.res])
            bc = bcs.next()
            S.op("act", lambda e: e.activation(out=bc.t[0:64, :], in_=pb.t[0:64, :], func=AF.Copy), reads=[pb.res], writes=[bc.res])
            os_ = ost.next()
            S.op("dve", lambda e: e.tensor_tensor(out=os_.t[0:64, :], in0=po.t[0:64, :], in1=bc.t[0:64, :], op=ALU.mult),
                 reads=[po.res, bc.res], writes=[os_.res])
            for h in range(4):
                row = (4 + h) * 64
                S.dma("sp", self.oT[row:row + 64, ts], os_.t[0:64, h * 128:(h + 1) * 128], reads=[os_.res])

        def drain(g):
            if g is not None:
                for _ in g:
                    pass

        def step(g, k):
            if g is None:
                return None
            try:
                for _ in range(k):
                    next(g)
            except StopIteration:
                return None
            return g

        drain(index_scores(0))
        for tb in range(NB):
            gi = index_scores(tb + 1) if tb + 1 < NB else None
            ga = attention(tb - 1) if tb >= 1 else None
            gt = threshold_mask(tb)
            n_i = 8 * ((tb + 2) * 128 + 511) // 512 if gi is not None else 0
            n_a = 4 * ((tb + 3) // 4) if ga is not None else 0
            rounds = NIT if tb >= 2 else 1
            k_i = -(-n_i // rounds)
            k_a = -(-n_a // rounds)
            while True:
                r = next(gt)
                if r == "final":
                    break
                gi = step(gi, k_i)
                ga = step(ga, k_a)
            drain(gi)
            drain(ga)
            drain(gt)
            if tb >= 1:
                finalize(tb - 1)
        drain(attention(NB - 1))
        finalize(NB - 1)


Prog.phase_dsa = phase_dsa


DIL_PATTERNS = ((128, 1), (512, 4), (2048, 16))


def phase_dil(self, l):
    S = self.S
    T, NB, NCK = self.T, self.NB, self.NC
    with ExitStack() as c:
        Qc = self.sbring(c, "dl_Q", [64, T], BF16, 2)
        Kc = self.sbring(c, "dl_K", [64, T], BF16, 2)
        Vr = [self.sbring(c, f"dl_V{r}", [128, NB, 65], BF16, 2) for (_, r) in DIL_PATTERNS]
        accT = Buf(self.sb(c, "dl_acc", [65, T], F32))
        dmask = Buf(self.sb(c, "dl_mask", [128, 2, 128], BF16))
        ones_row = Buf(self.sb(c, "dl_ones", [128, 64], F32))
        rrow = Buf(self.sb(c, "dl_rrow", [128, 512], F32))
        bcs = self.sbring(c, "dl_bcs", [64, 512], F32, 2)
        ost = self.sbring(c, "dl_ost", [64, 512], BF16, 2)
        pt = self.sbring(c, "dl_pt", [128, 256], BF16, 3)
        psS = self.psring(c, "dl_psS", [128, 512], F32, 3)
        psO = self.psring(c, "dl_psO", [128, 512], F32, 2)
        psB = self.psring(c, "dl_psB", [128, 512], F32, 1)
        S.dma("sp", dmask.t[:], self.cin["dilmask"][:, :, :], writes=[dmask.res])
        S.op("dve", lambda e: e.memset(ones_row.t[:], 1.0), writes=[ones_row.res])
        for ring in Vr:
            for b in ring.bufs:
                S.op("pool", lambda e, b=b: e.memset(b.t[:, :, 64:65], 1.0), writes=[b.res])
        for h in range(4):
            Q, K_ = Qc.next(), Kc.next()
            S.dma("sp", Q.t[:, :], self.fm[QC0 + h * 64:QC0 + (h + 1) * 64, :], writes=[Q.res])
            S.dma("sp", K_.t[:, :], self.fm[KC0 + h * 64:KC0 + (h + 1) * 64, :], writes=[K_.res])
            col = 512 + h * 64
            V = []
            for pi, (wdw, r) in enumerate(DIL_PATTERNS):
                v = Vr[pi].next()
                nn = T // (128 * r)
                for n in range(nn):
                    src = self.vtm[n * 128 * r:(n + 1) * 128 * r, col:col + 64].rearrange("(j rho) c -> j rho c", rho=r)
                    S.dma("sp", v.t[:, n * r:(n + 1) * r, 0:64], src, writes=[v.res])
                V.append(v)
            for pi, (wdw, r) in enumerate(DIL_PATTERNS):
                v = V[pi]
                nn = T // (128 * r)
                items = [(n, rho) for n in range(nn) for rho in range(r)]
                pend = {}

                def s1(it, n, rho, r=r):
                    st_ = n * 128 * r + rho
                    qcols = slice(st_, st_ + 127 * r + 1, r)
                    ps = psS.next()
                    lo = 0 if n >= 1 else 128
                    if n >= 1:
                        pcols = slice(st_ - 128 * r, st_ - r + 1, r)
                        S.op("pe", lambda e: e.matmul(ps.t[:, 0:128], lhsT=K_.t[:, pcols], rhs=Q.t[:, qcols], start=True, stop=False),
                             reads=[K_.res, Q.res], writes=[ps.res], signal=False)
                        S.op("pe", lambda e: e.matmul(ps.t[:, 0:128], lhsT=self.ident_bf.t[:], rhs=dmask.t[:, 0, :], start=False, stop=True),
                             reads=[self.ident_bf.res, dmask.res], writes=[ps.res], signal=False)
                    S.op("pe", lambda e: e.matmul(ps.t[:, 128:256], lhsT=K_.t[:, qcols], rhs=Q.t[:, qcols], start=True, stop=False),
                         reads=[K_.res, Q.res], writes=[ps.res], signal=False)
                    S.op("pe", lambda e: e.matmul(ps.t[:, 128:256], lhsT=self.ident_bf.t[:], rhs=dmask.t[:, 1, :], start=False, stop=True),
                         reads=[self.ident_bf.res, dmask.res], writes=[ps.res])
                    p = pt.next()
                    S.op("act", lambda e: e.activation(out=p.t[:, lo:256], in_=ps.t[:, lo:256], func=AF.Exp), reads=[ps.res], writes=[p.res])
                    pend[it] = p

                def s2(it, n, rho, r=r, pi=pi, v=v):
                    st_ = n * 128 * r + rho
                    qcols = slice(st_, st_ + 127 * r + 1, r)
                    p = pend.pop(it)
                    po = psO.next()
                    if n >= 1:
                        S.op("pe", lambda e: e.matmul(po.t[0:65, 0:128], lhsT=v.t[:, (n - 1) * r + rho, :], rhs=p.t[:, 0:128], start=True, stop=False),
                             reads=[v.res, p.res], writes=[po.res], signal=False)
                    S.op("pe", lambda e: e.matmul(po.t[0:65, 0:128], lhsT=v.t[:, n * r + rho, :], rhs=p.t[:, 128:256], start=(n == 0), stop=True),
                         reads=[v.res, p.res], writes=[po.res])
                    if pi == 0:
                        S.op("act", lambda e: e.activation(out=accT.t[0:65, qcols], in_=po.t[0:65, 0:128], func=AF.Copy),
                             reads=[po.res], writes=[accT.res])
                    else:
                        S.op("dve", lambda e: e.tensor_tensor(out=accT.t[0:65, qcols], in0=po.t[0:65, 0:128], in1=accT.t[0:65, qcols], op=ALU.add),
                             reads=[po.res, accT.res], writes=[accT.res])

                for it in range(len(items) + 1):
                    if it < len(items):
                        s1(it, *items[it])
                    if it >= 1:
                        s2(it - 1, *items[it - 1])
            for qc in range(NCK):
                qs = slice(qc * 512, (qc + 1) * 512)
                S.op("dve", lambda e, qs=qs: e.reciprocal(out=rrow.t[64:65, :], in_=accT.t[64:65, qs]), reads=[accT.res], writes=[rrow.res])
                pb = psB.next()
                S.op("pe", lambda e, pb=pb: e.matmul(pb.t[0:64, :], lhsT=ones_row.t[64:65, 0:64], rhs=rrow.t[64:65, :], start=True, stop=True),
                     reads=[ones_row.res, rrow.res], writes=[pb.res])
                os_ = ost.next()
                S.op("dve", lambda e, pb=pb, os_=os_, qs=qs: e.tensor_tensor(out=os_.t[0:64, :], in0=pb.t[0:64, :], in1=accT.t[0:64, qs], op=ALU.mult),
                     reads=[pb.res, accT.res], writes=[os_.res])
                row = (8 + h) * 64
                S.dma("sp", self.oT[row:row + 64, qs], os_.t[0:64, :], reads=[os_.res])


Prog.phase_dil = phase_dil


class LNCtx:
    def __init__(self, P, c, l, which, nb=2):
        S = P.S
        self.g = Buf(P.sb(c, "ln_g", [128, D], F32))
        self.b = Buf(P.sb(c, "ln_b", [128, D], F32))
        S.dma("sp", self.g.t[:], P.lnp[l, 2 * which, :].partition_broadcast(128), writes=[self.g.res])
        S.dma("sp", self.b.t[:], P.lnp[l, 2 * which + 1, :].partition_broadcast(128), writes=[self.b.res])
        self.xin = P.sbring(c, "ln_x", [128, D], F32, 2 if nb == 1 else 3)
        self.z = P.sbring(c, "ln_z", [128, D], F32, 2 if nb == 1 else 4)
        self.junk = P.sbring(c, "ln_junk", [128, D], BF16, 1)
        self.xn = P.sbring(c, "ln_xn", [128, D], F32, 2)
        self.xo = P.sbring(c, "ln_xo", [128, D], F32, 2)
        self.xb = P.sbring(c, "ln_xb", [128, D], BF16, 2)
        self.st = P.sbring(c, "ln_s", [128, 8], F32, 6)
        self.tp = P.psring(c, "ln_tp", [128, 8, 128], BF16, 1)
        self.tst = P.sbring(c, "ln_tst", [128, 8, 128], BF16, 2)


def _step_all(active):
    alive = []
    for g in active:
        try:
            next(g)
            alive.append(g)
        except StopIteration:
            pass
    active[:] = alive


def _drain_all(active):
    while active:
        _step_all(active)


def emit_res_ln(P, L, tb, py, xsrc, dst, make_xT):
    S = P.S
    rows = slice(tb * 128, (tb + 1) * 128)
    x = L.xin.next()
    S.dma("sp", x.t[:], xsrc[rows, :], writes=[x.res])
    z = L.z.next()
    S.op("dve", lambda e: e.scalar_tensor_tensor(out=z.t[:], in0=x.t[:], scalar=float(ALPHA), in1=py.t[:], op0=ALU.mult, op1=ALU.add),
         reads=[x.res, py.res], writes=[z.res])
    s = L.st.next()
    jk = L.junk.next()
    S.op("act", lambda e: e.activation(out=jk.t[:], in_=z.t[:], func=AF.Identity, accum_out=s.t[:, 0:1]),
         reads=[z.res], writes=[jk.res, s.res])
    S.op("act", lambda e: e.activation(out=jk.t[:], in_=z.t[:], func=AF.Square, accum_out=s.t[:, 1:2]),
         reads=[z.res], writes=[jk.res, s.res])
    yield
    S.op("dve", lambda e: e.tensor_scalar(out=s.t[:, 2:3], in0=s.t[:, 0:1], scalar1=1.0 / D, scalar2=None, op0=ALU.mult),
         reads=[s.res], writes=[s.res])
    S.op("dve", lambda e: e.tensor_tensor(out=s.t[:, 3:4], in0=s.t[:, 2:3], in1=s.t[:, 2:3], op=ALU.mult), reads=[s.res], writes=[s.res])
    S.op("dve", lambda e: e.scalar_tensor_tensor(out=s.t[:, 4:5], in0=s.t[:, 1:2], scalar=1.0 / D, in1=s.t[:, 3:4], op0=ALU.mult, op1=ALU.subtract),
         reads=[s.res], writes=[s.res])
    S.op("dve", lambda e: e.tensor_scalar(out=s.t[:, 4:5], in0=s.t[:, 4:5], scalar1=float(LN_EPS), scalar2=None, op0=ALU.add),
         reads=[s.res], writes=[s.res])
    S.op("act", lambda e: e.activation(out=s.t[:, 5:6], in_=s.t[:, 4:5], func=AF.Sqrt), reads=[s.res], writes=[s.res])
    yield
    S.op("dve", lambda e: e.reciprocal(out=s.t[:, 6:7], in_=s.t[:, 5:6]), reads=[s.res], writes=[s.res])
    S.op("dve", lambda e: e.scalar_tensor_tensor(out=s.t[:, 7:8], in0=s.t[:, 2:3], scalar=-1.0, in1=s.t[:, 6:7], op0=ALU.mult, op1=ALU.mult),
         reads=[s.res], writes=[s.res])
    xn = L.xn.next()
    S.op("act", lambda e: e.activation(out=xn.t[:], in_=z.t[:], func=AF.Identity, scale=s.t[:, 6:7], bias=s.t[:, 7:8]),
         reads=[z.res, s.res], writes=[xn.res])
    yield
    S.op("dve", lambda e: e.tensor_tensor(out=xn.t[:], in0=xn.t[:], in1=L.g.t[:], op=ALU.mult), reads=[xn.res, L.g.res], writes=[xn.res])
    xo = L.xo.next()
    S.op("dve", lambda e: e.tensor_tensor(out=xo.t[:], in0=xn.t[:], in1=L.b.t[:], op=ALU.add), reads=[xn.res, L.b.res], writes=[xo.res])
    S.dma("sp", dst[rows, :], xo.t[:], reads=[xo.res])
    if make_xT:
        xb = L.xb.next()
        S.op("act", lambda e: e.activation(out=xb.t[:], in_=xo.t[:], func=AF.Copy), reads=[xo.res], writes=[xb.res])
        yield
        P.emit_xT(tb, xb, L.tp, L.tst)


def phase_F(self, l, xres_in):
    S = self.S
    T, NB, NCK = self.T, self.NB, self.NC
    with ExitStack() as c:
        wo = Buf(self.sb(c, "F_wo", [128, 8, D], BF16))
        for ec in range(8):
            self.load_cast(c, wo, lambda c0, w, ec=ec: wo.t[:, ec, c0:c0 + w], self.w_o[l, ec * 128:(ec + 1) * 128, :], D, "F")
        L = LNCtx(self, c, l, 0)
        oTc = self.sbring(c, "F_oT", [128, 8, 512], BF16, 2)
        psY = self.psring(c, "F_psY", [128, 1024], F32, 2)
        active = []
        for ck in range(NCK):
            o = oTc.next()
            S.dma("sp", o.t[:], self.oT.rearrange("(ec p) t -> p ec t", p=128)[:, :, ck * 512:(ck + 1) * 512], writes=[o.res])
            for j in range(4):
                tb = ck * 4 + j
                py = psY.next()
                for half in range(2):
                    for ec in range(8):
                        S.op("pe", lambda e, ec=ec, half=half: e.matmul(py.t[:, half * 512:(half + 1) * 512], lhsT=o.t[:, ec, j * 128:(j + 1) * 128],
                                                                        rhs=wo.t[:, ec, half * 512:(half + 1) * 512], start=(ec == 0), stop=(ec == 7)),
                             reads=[o.res, wo.res], writes=[py.res], signal=(ec == 7))
                active.append(emit_res_ln(self, L, tb, py, xres_in, self.xres, True))
                _step_all(active)
        _drain_all(active)


Prog.phase_F = phase_F


def phase_G(self, l, last):
    S = self.S
    T, NB = self.T, self.NB
    CH = 256
    NJ = CH // 128
    with ExitStack() as c:
        wup = Buf(self.sb(c, "G_wup", [128, 8, DFF], BF16))
        wdn = Buf(self.sb(c, "G_wdn", [128, 32, D], BF16))
        for dc in range(8):
            self.load_cast(c, wup, lambda c0, w, dc=dc: wup.t[:, dc, c0:c0 + w], self.w_up[l, dc * 128:(dc + 1) * 128, :], DFF, "G")
        for fc in range(32):
            self.load_cast(c, wdn, lambda c0, w, fc=fc: wdn.t[:, fc, c0:c0 + w], self.w_down[l, fc * 128:(fc + 1) * 128, :], D, "G")
        L = LNCtx(self, c, l, 1, nb=1)
        xTc = self.sbring(c, "G_xT", [128, 8, CH], BF16, 2)
        hT = Buf(self.sb(c, "G_hT", [128, 16, CH], BF16))
        rr = self.sbring(c, "G_r", [128, CH], F32, 2)
        psU = self.psring(c, "G_psU", [128, 512], F32, 2)
        psY = self.psring(c, "G_psY", [128, 1024], F32, NJ)
        dst = self.y if last else self.xres
        active = []
        for ck in range(T // CH):
            xt = xTc.next()
            S.dma("sp", xt.t[:], self.xT.rearrange("(dc p) t -> p dc t", p=128)[:, :, ck * CH:(ck + 1) * CH], writes=[xt.res])
            pys = [psY.next() for _ in range(NJ)]
            for fh in range(2):
                for f16 in range(16):
                    fc = fh * 16 + f16
                    pu = psU.next()
                    for dc in range(8):
                        S.op("pe", lambda e, dc=dc, fc=fc, pu=pu: e.matmul(pu.t[:, 0:CH], lhsT=wup.t[:, dc, fc * 128:(fc + 1) * 128], rhs=xt.t[:, dc, :],
                                                                           start=(dc == 0), stop=(dc == 7)),
                             reads=[wup.res, xt.res], writes=[pu.res], signal=(dc == 7))
                    r = rr.next()
                    S.op("act", lambda e, pu=pu, r=r: e.activation(out=r.t[:], in_=pu.t[:, 0:CH], func=AF.Relu), reads=[pu.res], writes=[r.res])
                    S.op("dve", lambda e, r=r, f16=f16: e.tensor_tensor(out=hT.t[:, f16, :], in0=r.t[:], in1=r.t[:], op=ALU.mult),
                         reads=[r.res], writes=[hT.res])
                    if f16 % 6 == 5:
                        _step_all(active)
                for j in range(NJ):
                    py = pys[j]
                    for half in range(2):
                        for f16 in range(16):
                            fc = fh * 16 + f16
                            S.op("pe", lambda e, fc=fc, f16=f16, half=half, py=py, j=j: e.matmul(
                                py.t[:, half * 512:(half + 1) * 512], lhsT=hT.t[:, f16, j * 128:(j + 1) * 128],
                                rhs=wdn.t[:, fc, half * 512:(half + 1) * 512], start=(fc == 0), stop=(fc == 31)),
                                 reads=[hT.res, wdn.res], writes=[py.res], signal=(f16 == 15))
            _drain_all(active)
            for j in range(NJ):
                active.append(emit_res_ln(self, L, ck * NJ + j, pys[j], self.xres, dst, not last))
        _drain_all(active)


Prog.phase_G = phase_G


_CACHE = {}


def kernel(x, w_in, b_f, w_o, ln1_g, ln1_b, w_up, w_down, ln2_g, ln2_b):
    x = np.asarray(x, np.float32)
    B, T, _ = x.shape
    depth = int(np.asarray(w_in).shape[0])
    key = (T, depth)
    if key not in _CACHE:
        P = Prog(T=T, depth=depth)
        _CACHE[key] = P.build()
    nc = _CACHE[key]
    shared = prep_shared(T, depth, w_in, b_f, w_o, ln1_g, ln1_b, w_up, w_down, ln2_g, ln2_b)
    in_maps = []
    for cid in range(N_CORES):
        m = dict(shared)
        m["x"] = np.ascontiguousarray(x[cid % B])
        in_maps.append(m)
    res = run_bass_kernel_spmd(nc, in_maps, core_ids=list(range(N_CORES)))
    return np.stack([np.asarray(res.results[b]["y"], dtype=np.float32) for b in range(B)], axis=0)
```

```python
from contextlib import ExitStack
import numpy as np
import ml_dtypes
import concourse.bass as bass
import concourse.mybir as mybir
from concourse.bass_utils import run_bass_kernel_spmd

F32 = mybir.dt.float32
BF16 = mybir.dt.bfloat16
AF = mybir.ActivationFunctionType
ALU = mybir.AluOpType
AX = mybir.AxisListType

D = 1024
HD = 64
DFF = 4096
DEPTH = 4
SEQ = 8192
BATCH = 4
N_CORES = 8
ALPHA = (2.0 * DEPTH) ** 0.25
LN_EPS = 1e-5
NEG = -30000.0
DIN = 3372

QA0, KA0 = 0, 280
QB0, KB0 = 560, 816
QI0, KI0 = 1072, 1328
QC0, KC0 = 1360, 1616
QD0, KD0 = 1872, 2256
FM_ROWS = 2640


def _fm_chunks():
    ch = []

    def heads(col0, rope, scale, base, stride):
        for i in range(2):
            cols = list(range(col0 + i * 128, col0 + (i + 1) * 128))
            dst = [(0, 64, base + (2 * i) * stride), (64, 64, base + (2 * i + 1) * stride)]
            ch.append((cols, rope, scale, dst))

    heads(0, None, 0.125, QA0, 70)
    heads(256, None, 1.0, KA0, 70)
    ch.append((list(range(768, 772)), "fa", 1.0, []))
    heads(772, 64, 0.125, QB0, 64)
    heads(1028, 64, 1.0, KB0, 64)
    for i in range(2):
        ch.append((list(range(1540 + i * 128, 1540 + (i + 1) * 128)), 32, 1.0, [(0, 128, QI0 + i * 128)]))
    ch.append((list(range(1796, 1828)), 32, 1.0, [(0, 32, KI0)]))
    heads(1836, 64, 0.125, QC0, 64)
    heads(2092, 64, 1.0, KC0, 64)
    heads(2604, 64, 0.125, QD0, 96)
    heads(2860, 64, 1.0, KD0, 96)
    return ch


FM_CHUNKS = _fm_chunks()


def _swap_cols(cols, hd):
    out = []
    for i in range(0, len(cols), hd):
        blk = cols[i:i + hd]
        out += blk[hd // 2:] + blk[:hd // 2]
    return out


def _wfm_layout():
    cols = []
    offs = []
    for (c, rope, scale, dst) in FM_CHUNKS:
        a0 = len(cols)
        cols += c
        b0 = None
        if rope in (64, 32):
            b0 = len(cols)
            cols += _swap_cols(c, rope)
        offs.append((a0, b0, len(c)))
    return cols, offs


WFM_COLS, WFM_OFFS = _wfm_layout()
NWFM = len(WFM_COLS)
WTM_COLS = (list(range(512, 768)) + list(range(1284, 1540)) + list(range(2348, 2604)) +
            list(range(3116, 3372)) + list(range(1828, 1836)))
NWTM = len(WTM_COLS)


class Res:
    __slots__ = ("name", "w", "r")

    def __init__(self, name=""):
        self.name = name
        self.w = None
        self.r = {}


class Sched:
    def __init__(self, nc, n_dma_sems=40):
        self.nc = nc
        self.engs = {"pe": nc.tensor, "act": nc.scalar, "dve": nc.vector, "pool": nc.gpsimd, "sp": nc.sync}
        self.sem = {k: nc.alloc_semaphore(name="sem_" + k) for k in self.engs}
        self.cnt = {k: 0 for k in self.engs}
        self.seen = {k: {} for k in self.engs}
        self.dma_sems = [nc.alloc_semaphore(name=f"dsem{i}") for i in range(n_dma_sems)]
        self.dma_cnt = [0] * n_dma_sems
        self.dma_rr = 0
        self.n_inst = 0

    def _wait(self, eng, tok, war=False):
        if tok is None:
            return
        key, val = tok
        if val <= 0:
            return
        if key[0] == "e" and key[1] == eng and (eng == "pe" or eng == "sp" or war):
            return
        if self.seen[eng].get(key, 0) >= val:
            return
        h = self.sem[key[1]] if key[0] == "e" else self.dma_sems[key[1]]
        self.engs[eng].wait_ge(h, val)
        self.seen[eng][key] = val

    def _deps(self, eng, reads, writes):
        for r in reads:
            self._wait(eng, r.w)
        for w in writes:
            self._wait(eng, w.w)
            for k, v in w.r.items():
                self._wait(eng, (k, v), war=True)

    def _update(self, tok, reads, writes):
        k, v = tok
        for r in reads:
            if r.r.get(k, 0) < v:
                r.r[k] = v
        for w in writes:
            w.w = tok
            w.r = {}

    def op(self, eng, fn, reads=(), writes=(), signal=True):
        self._deps(eng, reads, writes)
        inst = fn(self.engs[eng])
        self.n_inst += 1
        if signal:
            self.cnt[eng] += 1
            inst.then_inc(self.sem[eng], 1)
            tok = (("e", eng), self.cnt[eng])
        else:
            tok = (("e", eng), self.cnt[eng] + 1)
        self._update(tok, reads, writes)
        return tok

    def dma(self, eng, out, in_, reads=(), writes=(), **kw):
        i = self.dma_rr
        self.dma_rr = (i + 1) % len(self.dma_sems)
        self._wait(eng, (("d", i), self.dma_cnt[i]))
        self._deps(eng, reads, writes)
        inst = self.engs[eng].dma_start(out=out, in_=in_, **kw)
        self.n_inst += 1
        self.dma_cnt[i] += 16
        inst.then_inc(self.dma_sems[i], 16)
        tok = (("d", i), self.dma_cnt[i])
        self._update(tok, reads, writes)
        return tok

    def barrier(self):
        toks = [(("e", k), self.cnt[k]) for k in self.engs] + \
               [(("d", i), self.dma_cnt[i]) for i in range(len(self.dma_sems))]
        for e in self.engs:
            for t in toks:
                self._wait(e, t)


def host_consts(T):
    c = {}
    pos = np.arange(T, dtype=np.float32)
    for hd, nm in ((64, "64"), (32, "32")):
        half = hd // 2
        inv = (10000.0 ** (-np.arange(half, dtype=np.float32) / half)).astype(np.float32)
        ang = pos[None, :] * inv[:, None]
        cos = np.cos(ang).astype(np.float32)
        sin = np.sin(ang).astype(np.float32)
        rows_c, rows_s = [], []
        for p in range(128):
            f = p % half
            rows_c.append(cos[f])
            rows_s.append(-sin[f] if (p % hd) < half else sin[f])
        c["cos" + nm] = np.stack(rows_c).astype(np.float32)
        c["sin" + nm] = np.stack(rows_s).astype(np.float32)
    oh = np.zeros((32, T), np.float32)
    for n in range(32):
        oh[n, n * 256:(n + 1) * 256] = 1.0
    c["onehotT"] = oh.astype(ml_dtypes.bfloat16)
    c["onesT"] = np.ones((3, T), np.float32).astype(ml_dtypes.bfloat16)
    c["ident_bf"] = np.eye(128, dtype=np.float32).astype(ml_dtypes.bfloat16)
    c["ident_f32"] = np.eye(128, dtype=np.float32)
    p = np.arange(128)[:, None]
    f = np.arange(512)[None, :]
    dm = np.zeros((128, 4, 512), np.float32)
    for m in range(4):
        dm[:, m, :] = np.where(f - p - 128 * m >= 0, 0.0, NEG)
    c["diagmask"] = dm.astype(ml_dtypes.bfloat16)
    f1 = np.arange(128)[None, :]
    dil = np.zeros((128, 2, 128), np.float32)
    dil[:, 0, :] = np.where(p >= f1, 0.0, NEG)
    dil[:, 1, :] = np.where(p <= f1, 0.0, NEG)
    c["dilmask"] = dil.astype(ml_dtypes.bfloat16)
    c["tri"] = np.where(f1 <= p, 0.0, -1e30).astype(np.float32)
    c["pow2"] = np.broadcast_to((0.5 ** np.arange(1, 20, dtype=np.float64)).astype(np.float32)[None], (128, 19)).copy()
    own = np.arange(32)[:, None]
    n = np.arange(32)[None, :]
    pb = np.where(n < own, 0.0, -1e30).astype(np.float32)
    pm = (n < own).astype(np.float32)
    om = (n == own).astype(np.float32)
    c["moba_pb"] = np.broadcast_to(np.tile(pb, (1, 4))[None], (128, 32, 128)).copy()
    c["moba_pm"] = np.broadcast_to(np.tile(pm, (1, 4))[None], (128, 32, 128)).copy()
    c["moba_om"] = np.broadcast_to(np.tile(om, (1, 4))[None], (128, 32, 128)).copy()
    return c


class Buf:
    __slots__ = ("t", "res")

    def __init__(self, t, name=""):
        self.t = t
        self.res = Res(name)


class Ring:
    def __init__(self, bufs):
        self.bufs = bufs
        self.i = 0

    def next(self):
        b = self.bufs[self.i]
        self.i = (self.i + 1) % len(self.bufs)
        return b


class Prog:
    def __init__(self, T=SEQ, depth=DEPTH, debug=False, phases=None):
        self.T = T
        self.NB = T // 128
        self.NC = T // 512
        self.depth = depth
        self.debug = debug
        self.phases = phases
        nc = bass.Bass("TRN2", target_bir_lowering=False)
        self.nc = nc
        self.S = Sched(nc)
        dk = "ExternalOutput" if debug else "Internal"
        self.x_in = nc.dram_tensor("x", [T, D], F32, kind="ExternalInput").ap()
        self.w_fm = nc.dram_tensor("w_fm", [depth, D, NWFM], F32, kind="ExternalInput").ap()
        self.w_tm = nc.dram_tensor("w_tm", [depth, D, NWTM], F32, kind="ExternalInput").ap()
        self.b_f = nc.dram_tensor("b_f", [depth, 4, 1], F32, kind="ExternalInput").ap()
        self.w_o = nc.dram_tensor("w_o", [depth, D, D], F32, kind="ExternalInput").ap()
        self.w_up = nc.dram_tensor("w_up", [depth, D, DFF], F32, kind="ExternalInput").ap()
        self.w_down = nc.dram_tensor("w_down", [depth, DFF, D], F32, kind="ExternalInput").ap()
        self.lnp = nc.dram_tensor("lnp", [depth, 4, D], F32, kind="ExternalInput").ap()
        cs = host_consts(T)
        self.cin = {}
        for k, v in cs.items():
            dt = BF16 if v.dtype == ml_dtypes.bfloat16 else F32
            self.cin[k] = nc.dram_tensor("c_" + k, list(v.shape), dt, kind="ExternalInput").ap()
        self.y = nc.dram_tensor("y", [T, D], F32, kind="ExternalOutput").ap()
        self.xres = nc.dram_tensor("xres", [T, D], F32, kind=dk).ap()
        self.xT = nc.dram_tensor("xT", [D, T], BF16, kind=dk).ap()
        self.fm = nc.dram_tensor("fm", [FM_ROWS, T], BF16, kind=dk).ap()
        self.vtm = nc.dram_tensor("vtm", [T, 1024], BF16, kind=dk).ap()
        self.oT = nc.dram_tensor("oT", [D, T], BF16, kind=dk).ap()

    def _uniq(self, name):
        self._uid = getattr(self, "_uid", 0) + 1
        return f"{name}_{self._uid}"

    def sb(self, ctx, name, shape, dt):
        return ctx.enter_context(self.nc.sbuf_tensor(self._uniq(name), shape, dt))

    def ps(self, ctx, name, shape, dt=F32):
        return ctx.enter_context(self.nc.psum_tensor(self._uniq(name), shape, dt))

    def sbring(self, ctx, name, shape, dt, n):
        return Ring([Buf(self.sb(ctx, f"{name}{i}", shape, dt), f"{name}{i}") for i in range(n)])

    def psring(self, ctx, name, shape, dt, n):
        return Ring([Buf(self.ps(ctx, f"{name}{i}", shape, dt), f"{name}{i}") for i in range(n)])

    def load_cast(self, ctx, dst, dst_slices, src_rows, ncols, name):
        S = self.S
        if not hasattr(self, "_stg") or self._stg_ctx is not ctx:
            self._stg = self.sbring(ctx, "wstage", [128, 1024], F32, 2)
            self._stg_ctx = ctx
            self._stg_flip = 0
        for c0 in range(0, ncols, 1024):
            w = min(1024, ncols - c0)
            st = self._stg.next()
            S.dma("sp", st.t[:, 0:w], src_rows[:, c0:c0 + w], writes=[st.res])
            out_ap = dst_slices(c0, w)
            if self._stg_flip % 2 == 0:
                S.op("act", lambda e: e.activation(out=out_ap, in_=st.t[:, 0:w], func=AF.Copy), reads=[st.res], writes=[dst.res])
            else:
                S.op("dve", lambda e: e.tensor_copy(out=out_ap, in_=st.t[:, 0:w]), reads=[st.res], writes=[dst.res])
            self._stg_flip += 1

    def want(self, ph):
        return self.phases is None or ph in self.phases

    def build(self):
        S = self.S
        with ExitStack() as g:
            self.ident_bf = Buf(self.sb(g, "ident_bf", [128, 128], BF16))
            self.ident_f = Buf(self.sb(g, "ident_f", [128, 128], F32))
            self.witm = Buf(self.sb(g, "witm", [128, self.NB, 8], F32))
            S.dma("sp", self.ident_bf.t[:], self.cin["ident_bf"][:, :], writes=[self.ident_bf.res])
            S.dma("sp", self.ident_f.t[:], self.cin["ident_f32"][:, :], writes=[self.ident_f.res])
            for h in range(4):
                S.dma("sp", self.fm[QA0 + h * 70 + 67:QA0 + h * 70 + 70, :], self.cin["onesT"][:, :])
                S.dma("sp", self.fm[KA0 + h * 70 + 64:KA0 + h * 70 + 67, :], self.cin["onesT"][:, :])
                S.dma("sp", self.fm[KD0 + h * 96 + 64:KD0 + h * 96 + 96, :], self.cin["onehotT"][:, :])
            S.barrier()
            if self.want("pro"):
                self.phase_prologue()
                S.barrier()
            for l in range(self.depth):
                xres_in = self.x_in if l == 0 else self.xres
                if self.want("A"):
                    self.phase_A(l)
                    S.barrier()
                    if self.debug:
                        wd = self.nc.dram_tensor(f"witm_dbg{l}", [128, self.NB, 8], F32, kind="ExternalOutput").ap()
                        S.dma("sp", wd[:, :, :], self.witm.t[:], reads=[self.witm.res])
                        S.barrier()
                if self.want("moba_prep"):
                    self.phase_moba_prep(l)
                    S.barrier()
                if self.want("fox"):
                    self.phase_dense(l, "fox")
                    S.barrier()
                if self.want("moba"):
                    self.phase_dense(l, "moba")
                    S.barrier()
                if self.want("dsa"):
                    self.phase_dsa(l)
                    S.barrier()
                if self.want("dil"):
                    self.phase_dil(l)
                    S.barrier()
                if self.want("F"):
                    self.phase_F(l, xres_in)
                    S.barrier()
                if self.want("G"):
                    self.phase_G(l, last=(l == self.depth - 1))
                    S.barrier()
        return self.nc

    def emit_xT(self, tb, src, tp_ring, st_ring):
        S = self.S
        tp = tp_ring.next()
        for dc in range(8):
            S.op("pe", lambda e, dc=dc: e.transpose(out=tp.t[:, dc, :], in_=src.t[:, dc * 128:(dc + 1) * 128],
                                                   identity=self.ident_bf.t[:]),
                 reads=[src.res, self.ident_bf.res], writes=[tp.res], signal=(dc == 7))
        st = st_ring.next()
        S.op("act", lambda e: e.activation(out=st.t[:], in_=tp.t[:], func=AF.Copy),
             reads=[tp.res], writes=[st.res])
        dst = self.xT.rearrange("(dc p) t -> p dc t", p=128)[:, :, tb * 128:(tb + 1) * 128]
        S.dma("sp", dst, st.t[:], reads=[st.res])

    def phase_prologue(self):
        S = self.S
        with ExitStack() as c:
            xin = self.sbring(c, "pro_x", [128, D], F32, 2)
            xbf = self.sbring(c, "pro_xb", [128, D], BF16, 2)
            tp = self.psring(c, "pro_tp", [128, 8, 128], BF16, 2)
            st = self.sbring(c, "pro_st", [128, 8, 128], BF16, 2)
            for tb in range(self.NB):
                a = xin.next()
                S.dma("sp", a.t[:], self.x_in[tb * 128:(tb + 1) * 128, :], writes=[a.res])
                b = xbf.next()
                S.op("dve", lambda e: e.tensor_copy(out=b.t[:], in_=a.t[:]), reads=[a.res], writes=[b.res])
                self.emit_xT(tb, b, tp, st)

    def phase_A(self, l):
        S = self.S
        nc = self.nc
        T = self.T
        with ExitStack() as c:
            wfm = Buf(self.sb(c, "A_wfm", [128, 8, NWFM], BF16))
            wtm = Buf(self.sb(c, "A_wtm", [128, 8, NWTM], BF16))
            nbf = Buf(self.sb(c, "A_nbf", [4, 1], F32))
            bfr = Buf(self.sb(c, "A_bf", [4, 1], F32))
            for dc in range(8):
                self.load_cast(c, wfm, lambda c0, w, dc=dc: wfm.t[:, dc, c0:c0 + w], self.w_fm[l, dc * 128:(dc + 1) * 128, :], NWFM, "A")
                self.load_cast(c, wtm, lambda c0, w, dc=dc: wtm.t[:, dc, c0:c0 + w], self.w_tm[l, dc * 128:(dc + 1) * 128, :], NWTM, "A")
            S.dma("sp", bfr.t[:], self.b_f[l], writes=[bfr.res])
            S.op("dve", lambda e: e.tensor_scalar(out=nbf.t[:], in0=bfr.t[:], scalar1=-1.0, scalar2=None, op0=ALU.mult),
                 reads=[bfr.res], writes=[nbf.res])
            xTc = self.sbring(c, "A_xT", [128, 8, 512], BF16, 2)
            tabs = {k: self.sbring(c, "A_" + k, [128, 512], F32, 2) for k in ("cos64", "sin64", "cos32", "sin32")}
            psA = self.psring(c, "A_psA", [128, 512], F32, 2)
            psB = self.psring(c, "A_psB", [128, 512], F32, 2)
            psV = self.psring(c, "A_psV", [128, 1024], F32, 1)
            psW = self.psring(c, "A_psW", [128, 8], F32, 1)
            t1r = self.sbring(c, "A_t1", [128, 512], F32, 2)
            t2r = self.sbring(c, "A_t2", [128, 512], F32, 2)
            stg = self.sbring(c, "A_stg", [128, 512], BF16, 3)
            vst = self.sbring(c, "A_vst", [128, 1024], BF16, 2)
            e_t = self.sbring(c, "A_e", [4, 512], F32, 2)
            ones4 = Buf(self.sb(c, "A_ones4", [4, 512], F32))
            c_t = self.sbring(c, "A_c", [4, 512], F32, 2)
            r_t = self.sbring(c, "A_r", [4, 512], F32, 2)
            p6 = self.sbring(c, "A_p6", [4, 6, 512], BF16, 2)
            n6 = self.sbring(c, "A_n6", [4, 6, 512], BF16, 2)
            S.op("dve", lambda e: e.memset(ones4.t[:], 1.0), writes=[ones4.res])
            c_prev = None
            for ck in range(self.NC):
                cs = slice(ck * 512, (ck + 1) * 512)
                xt = xTc.next()
                S.dma("sp", xt.t[:], self.xT.rearrange("(dc p) t -> p dc t", p=128)[:, :, cs], writes=[xt.res])
                tb_ = {}
                for k in tabs:
                    tb_[k] = tabs[k].next()
                    S.dma("sp", tb_[k].t[:], self.cin[k][:, cs], writes=[tb_[k].res])
                for gi, (cols, rope, scale, dst) in enumerate(FM_CHUNKS):
                    a0, b0, n = WFM_OFFS[gi]
                    pa = psA.next()
                    for dc in range(8):
                        S.op("pe", lambda e, dc=dc: e.matmul(pa.t[0:n, :], lhsT=wfm.t[:, dc, a0:a0 + n], rhs=xt.t[:, dc, :],
                                                             start=(dc == 0), stop=(dc == 7)),
                             reads=[wfm.res, xt.res], writes=[pa.res], signal=(dc == 7))
                    if rope == "fa":
                        e1 = e_t.next()
                        S.op("act", lambda e: e.activation(out=e1.t[:], in_=pa.t[0:4, :], func=AF.Exp, bias=nbf.t[:], scale=-1.0),
                             reads=[pa.res, nbf.res], writes=[e1.res])
                        S.op("act", lambda e: e.activation(out=e1.t[:], in_=e1.t[:], func=AF.Ln, bias=1.0, scale=1.0),
                             reads=[e1.res], writes=[e1.res])
                        cc = c_t.next()
                        init = 0.0 if c_prev is None else c_prev.t[:, 511:512]
                        rd = [ones4.res, e1.res] + ([] if c_prev is None else [c_prev.res])
                        S.op("dve", lambda e: e.tensor_tensor_scan(out=cc.t[:], data0=ones4.t[:], data1=e1.t[:], initial=init,
                                                                   op0=ALU.mult, op1=ALU.subtract),
                             reads=rd, writes=[cc.res])
                        c_prev = cc
                        P = p6.next()
                        N6 = n6.next()
                        R = r_t.next()
                        S.op("dve", lambda e: e.tensor_copy(out=P.t[:, 0, :], in_=cc.t[:]), reads=[cc.res], writes=[P.res])
                        S.op("dve", lambda e: e.tensor_tensor(out=R.t[:], in0=cc.t[:], in1=P.t[:, 0, :], op=ALU.subtract),
                             reads=[cc.res, P.res], writes=[R.res])
                        S.op("dve", lambda e: e.tensor_copy(out=P.t[:, 1, :], in_=R.t[:]), reads=[R.res], writes=[P.res])
                        S.op("dve", lambda e: e.tensor_tensor(out=R.t[:], in0=R.t[:], in1=P.t[:, 1, :], op=ALU.subtract),
                             reads=[R.res, P.res], writes=[R.res])
                        S.op("dve", lambda e: e.tensor_copy(out=P.t[:, 2, :], in_=R.t[:]), reads=[R.res], writes=[P.res])
                        S.op("dve", lambda e: e.tensor_scalar(out=N6.t[:, 0:3, :], in0=P.t[:, 0:3, :], scalar1=-1.0, scalar2=None,
                                                              op0=ALU.mult), reads=[P.res], writes=[N6.res])
                        qdst = self.fm[QA0:QA0 + 280, cs].rearrange("(h r) t -> h r t", r=70)[:, 64:67, :]
                        kdst = self.fm[KA0:KA0 + 280, cs].rearrange("(h r) t -> h r t", r=70)[:, 67:70, :]
                        S.dma("sp", qdst, P.t[:, 0:3, :], reads=[P.res])
                        S.dma("sp", kdst, N6.t[:, 0:3, :], reads=[N6.res])
                        continue
                    sg = stg.next()
                    if rope is None:
                        S.op("act", lambda e: e.activation(out=sg.t[0:n, :], in_=pa.t[0:n, :], func=AF.Copy, scale=float(scale)),
                             reads=[pa.res], writes=[sg.res])
                    else:
                        pb = psB.next()
                        for dc in range(8):
                            S.op("pe", lambda e, dc=dc: e.matmul(pb.t[0:n, :], lhsT=wfm.t[:, dc, b0:b0 + n], rhs=xt.t[:, dc, :],
                                                                 start=(dc == 0), stop=(dc == 7)),
                                 reads=[wfm.res, xt.res], writes=[pb.res], signal=(dc == 7))
                        ct = tb_["cos%d" % rope]
                        st_ = tb_["sin%d" % rope]
                        t1 = t1r.next()
                        t2 = t2r.next()
                        S.op("dve", lambda e: e.scalar_tensor_tensor(out=t1.t[0:n, :], in0=pa.t[0:n, :], scalar=float(scale),
                                                                     in1=ct.t[0:n, :], op0=ALU.mult, op1=ALU.mult),
                             reads=[pa.res, ct.res], writes=[t1.res])
                        S.op("dve", lambda e: e.scalar_tensor_tensor(out=t2.t[0:n, :], in0=pb.t[0:n, :], scalar=float(scale),
                                                                     in1=st_.t[0:n, :], op0=ALU.mult, op1=ALU.mult),
                             reads=[pb.res, st_.res], writes=[t2.res])
                        S.op("dve", lambda e: e.tensor_tensor(out=sg.t[0:n, :], in0=t1.t[0:n, :], in1=t2.t[0:n, :], op=ALU.add),
                             reads=[t1.res, t2.res], writes=[sg.res])
                    for (ro, nr, frow) in dst:
                        S.dma("sp", self.fm[frow:frow + nr, cs], sg.t[ro:ro + nr, :], reads=[sg.res])
                for j in range(4):
                    tb = ck * 4 + j
                    ts = slice(j * 128, (j + 1) * 128)
                    pv = psV.next()
                    pw = psW.next()
                    for half in range(2):
                        for dc in range(8):
                            S.op("pe", lambda e, dc=dc, half=half: e.matmul(pv.t[:, half * 512:(half + 1) * 512],
                                                                            lhsT=xt.t[:, dc, ts],
                                                                            rhs=wtm.t[:, dc, half * 512:(half + 1) * 512],
                                                                            start=(dc == 0), stop=(dc == 7)),
                                 reads=[wtm.res, xt.res], writes=[pv.res], signal=(dc == 7))
                    for dc in range(8):
                        S.op("pe", lambda e, dc=dc: e.matmul(pw.t[:, :], lhsT=xt.t[:, dc, ts], rhs=wtm.t[:, dc, 1024:1032],
                                                             start=(dc == 0), stop=(dc == 7)),
                             reads=[wtm.res, xt.res], writes=[pw.res], signal=(dc == 7))
                    vs = vst.next()
                    S.op("act", lambda e: e.activation(out=vs.t[:], in_=pv.t[:], func=AF.Copy), reads=[pv.res], writes=[vs.res])
                    S.dma("sp", self.vtm[tb * 128:(tb + 1) * 128, :], vs.t[:], reads=[vs.res])
                    S.op("dve", lambda e, tb=tb: e.tensor_copy(out=self.witm.t[:, tb, :], in_=pw.t[:, :]),
                         reads=[pw.res], writes=[self.witm.res])


def prep_shared(T, depth, w_in, b_f, w_o, ln1_g, ln1_b, w_up, w_down, ln2_g, ln2_b):
    m = {}
    w_in = np.asarray(w_in, np.float32)
    m["w_fm"] = np.ascontiguousarray(w_in[:depth][:, :, WFM_COLS])
    m["w_tm"] = np.ascontiguousarray(w_in[:depth][:, :, WTM_COLS])
    m["b_f"] = np.ascontiguousarray(np.asarray(b_f, np.float32)[:depth].reshape(depth, 4, 1))
    m["w_o"] = np.ascontiguousarray(np.asarray(w_o, np.float32)[:depth])
    m["w_up"] = np.ascontiguousarray(np.asarray(w_up, np.float32)[:depth])
    m["w_down"] = np.ascontiguousarray(np.asarray(w_down, np.float32)[:depth])
    m["lnp"] = np.ascontiguousarray(np.stack([np.asarray(a, np.float32)[:depth] for a in (ln1_g, ln1_b, ln2_g, ln2_b)], axis=1))
    for k, v in host_consts(T).items():
        m["c_" + k] = v
    return m


def _finalize(P, S, po, n, rrow, ones_row, psB, bcs_ring, ost_ring, dst):
    S.op("dve", lambda e: e.reciprocal(out=rrow.t[64:65, 0:n], in_=po.t[64:65, 0:n]), reads=[po.res], writes=[rrow.res])
    pb = psB.next()
    S.op("pe", lambda e: e.matmul(pb.t[0:64, 0:n], lhsT=ones_row.t[64:65, 0:64], rhs=rrow.t[64:65, 0:n], start=True, stop=True),
         reads=[ones_row.res, rrow.res], writes=[pb.res])
    bc = bcs_ring.next()
    S.op("act", lambda e: e.activation(out=bc.t[0:64, 0:n], in_=pb.t[0:64, 0:n], func=AF.Copy), reads=[pb.res], writes=[bc.res])
    os_ = ost_ring.next()
    S.op("dve", lambda e: e.tensor_tensor(out=os_.t[0:64, 0:n], in0=po.t[0:64, 0:n], in1=bc.t[0:64, 0:n], op=ALU.mult),
         reads=[po.res, bc.res], writes=[os_.res])
    S.dma("sp", dst, os_.t[0:64, 0:n], reads=[os_.res])


def phase_dense(self, l, kind):
    S = self.S
    T, NB, NCK = self.T, self.NB, self.NC
    KR = 70 if kind == "fox" else 96
    q0, k0 = (QA0, KA0) if kind == "fox" else (QD0, KD0)
    g = 0 if kind == "fox" else 3
    with ExitStack() as c:
        Qp = self.sbring(c, "dn_Q", [96, T], BF16, 2)
        Kp = self.sbring(c, "dn_K", [96, T], BF16, 2)
        Vp = self.sbring(c, "dn_V", [128, NB, 65], BF16, 2)
        dmask = Buf(self.sb(c, "dn_mask", [128, 4, 512], BF16))
        ones_row = Buf(self.sb(c, "dn_ones", [128, 64], F32))
        rrow = Buf(self.sb(c, "dn_rrow", [128, 512], F32))
        bcs = self.sbring(c, "dn_bcs", [64, 512], F32, 2)
        ost = self.sbring(c, "dn_ost", [64, 512], BF16, 2)
        pt = self.sbring(c, "dn_pt", [128, 512], BF16, 5)
        psS = self.psring(c, "dn_psS", [128, 512], F32, 4)
        psO = self.psring(c, "dn_psO", [128, 512], F32, 2)
        psB = self.psring(c, "dn_psB", [128, 512], F32, 1)
        S.dma("sp", dmask.t[:], self.cin["diagmask"][:, :, :], writes=[dmask.res])
        S.op("dve", lambda e: e.memset(ones_row.t[:], 1.0), writes=[ones_row.res])
        for b in Vp.bufs:
            S.op("pool", lambda e, b=b: e.memset(b.t[:, :, 64:65], 1.0), writes=[b.res])
        for h in range(4):
            Q, K_, V = Qp.next(), Kp.next(), Vp.next()
            S.dma("sp", Q.t[0:KR, :], self.fm[q0 + h * KR:q0 + (h + 1) * KR, :], writes=[Q.res])
            S.dma("sp", K_.t[0:KR, :], self.fm[k0 + h * KR:k0 + (h + 1) * KR, :], writes=[K_.res])
            col = g * 256 + h * 64
            S.dma("sp", V.t[:, :, 0:64], self.vtm.rearrange("(nb p) c -> p nb c", p=128)[:, :, col:col + 64], writes=[V.res])
            items = [(qc, j) for qc in range(NCK) for j in range(4 * qc + 4)]
            LA = 3
            pos = {}
            pq = {}

            def stage1(it, qc, j):
                qs = slice(qc * 512, (qc + 1) * 512)
                if j == 0:
                    pos[qc] = psO.next()
                ps = psS.next()
                m = j - 4 * qc
                S.op("pe", lambda e: e.matmul(ps.t[:, :], lhsT=K_.t[0:KR, j * 128:(j + 1) * 128], rhs=Q.t[0:KR, qs],
                                              start=True, stop=(m < 0)),
                     reads=[K_.res, Q.res], writes=[ps.res], signal=(m < 0))
                if m >= 0:
                    S.op("pe", lambda e: e.matmul(ps.t[:, :], lhsT=self.ident_bf.t[:], rhs=dmask.t[:, m, :], start=False, stop=True),
                         reads=[self.ident_bf.res, dmask.res], writes=[ps.res])
                p = pt.next()
                S.op("act", lambda e: e.activation(out=p.t[:], in_=ps.t[:], func=AF.Exp), reads=[ps.res], writes=[p.res])
                pq[it] = p

            def stage2(it, qc, j):
                nj = 4 * qc + 4
                po = pos[qc]
                p = pq.pop(it)
                S.op("pe", lambda e: e.matmul(po.t[0:65, :], lhsT=V.t[:, j, 0:65], rhs=p.t[:], start=(j == 0), stop=(j == nj - 1)),
                     reads=[V.res, p.res], writes=[po.res], signal=(j == nj - 1))
                if j == nj - 1:
                    row = (g * 4 + h) * 64
                    _finalize(self, S, po, 512, rrow, ones_row, psB, bcs, ost, self.oT[row:row + 64, qc * 512:(qc + 1) * 512])

            for it in range(len(items) + LA):
                if it < len(items):
                    stage1(it, *items[it])
                if it - LA >= 0:
                    stage2(it - LA, *items[it - LA])


Prog.phase_dense = phase_dense


def phase_moba_prep(self, l):
    S = self.S
    T, NB = self.T, self.NB
    nkb = T // 256
    with ExitStack() as c:
        Kd = [Buf(self.sb(c, f"mp_K{i}", [64, T], BF16)) for i in range(4)]
        Qd = [Buf(self.sb(c, f"mp_Q{i}", [64, T], BF16)) for i in range(4)]
        kmf = [Buf(self.sb(c, f"mp_kmf{i}", [64, 32], F32)) for i in range(4)]
        kmb = [Buf(self.sb(c, f"mp_kmb{i}", [64, 32], BF16)) for i in range(4)]
        tabs = {k: Buf(self.sb(c, "mp_" + k, [128, 32, 128], F32)) for k in ("moba_pb", "moba_pm", "moba_om")}
        for k in tabs:
            S.dma("sp", tabs[k].t[:], self.cin[k][:, :, :], writes=[tabs[k].res])
        gt = self.sbring(c, "mp_g", [128, 128], F32, 2)
        m8 = self.sbring(c, "mp_m8", [128, 4, 8], F32, 2)
        At = self.sbring(c, "mp_A", [128, 128], F32, 2)
        mbf = self.sbring(c, "mp_mbf", [128, 128], F32, 2)
        stg = self.sbring(c, "mp_stg", [128, 512], BF16, 2)
        psG = self.psring(c, "mp_psG", [128, 128], F32, 2)
        psT = self.psring(c, "mp_psT", [128, 512], F32, 2)
        for h in range(4):
            S.dma("sp", Kd[h].t[:, :], self.fm[KD0 + h * 96:KD0 + h * 96 + 64, :], writes=[Kd[h].res])
            S.dma("sp", Qd[h].t[:, :], self.fm[QD0 + h * 96:QD0 + h * 96 + 64, :], writes=[Qd[h].res])
            S.op("dve", lambda e, h=h: e.memset(kmf[h].t[:], 0.0), writes=[kmf[h].res])
            S.op("dve", lambda e, h=h: e.tensor_reduce(out=kmf[h].t[:, 0:nkb], in_=Kd[h].t[:].rearrange("p (n k) -> p n k", k=256),
                                                      axis=AX.X, op=ALU.add), reads=[Kd[h].res], writes=[kmf[h].res])
            S.op("dve", lambda e, h=h: e.tensor_scalar(out=kmb[h].t[:], in0=kmf[h].t[:], scalar1=1.0 / 256, scalar2=None, op0=ALU.mult),
                 reads=[kmf[h].res], writes=[kmb[h].res])
        for tb4 in range(NB // 4):
            pT = psT.next()
            for j in range(4):
                tb = tb4 * 4 + j
                own = tb // 2
                ts = slice(tb * 128, (tb + 1) * 128)
                pg = psG.next()
                for h in range(4):
                    S.op("pe", lambda e, h=h: e.matmul(pg.t[:, h * 32:(h + 1) * 32], lhsT=Qd[h].t[:, ts],
                                                       rhs=kmb[h].t[:, :], start=True, stop=True),
                         reads=[Qd[h].res, kmb[h].res], writes=[pg.res], signal=(h == 3))
                gg = gt.next()
                S.op("dve", lambda e: e.tensor_tensor(out=gg.t[:], in0=pg.t[:], in1=tabs["moba_pb"].t[:, own, :], op=ALU.add),
                     reads=[pg.res, tabs["moba_pb"].res], writes=[gg.res])
                mm = m8.next()
                for h in range(4):
                    S.op("dve", lambda e, h=h: e.max(out=mm.t[:, h, :], in_=gg.t[:, h * 32:(h + 1) * 32]), reads=[gg.res], writes=[mm.res])
                A = At.next()
                for h in range(4):
                    S.op("dve", lambda e, h=h: e.tensor_scalar(out=A.t[:, h * 32:(h + 1) * 32], in0=gg.t[:, h * 32:(h + 1) * 32],
                                                               scalar1=mm.t[:, h, 2:3], scalar2=None, op0=ALU.is_ge),
                         reads=[gg.res, mm.res], writes=[A.res])
                S.op("dve", lambda e: e.tensor_tensor(out=A.t[:], in0=A.t[:], in1=tabs["moba_pm"].t[:, own, :], op=ALU.mult),
                     reads=[A.res, tabs["moba_pm"].res], writes=[A.res])
                S.op("dve", lambda e: e.tensor_tensor(out=A.t[:], in0=A.t[:], in1=tabs["moba_om"].t[:, own, :], op=ALU.add),
                     reads=[A.res, tabs["moba_om"].res], writes=[A.res])
                mb = mbf.next()
                S.op("dve", lambda e: e.tensor_scalar(out=mb.t[:], in0=A.t[:], scalar1=-1.0, scalar2=-NEG, op0=ALU.add, op1=ALU.mult),
                     reads=[A.res], writes=[mb.res])
                S.op("pe", lambda e, j=j: e.transpose(out=pT.t[:, j * 128:(j + 1) * 128], in_=mb.t[:], identity=self.ident_f.t[:]),
                     reads=[mb.res, self.ident_f.res], writes=[pT.res])
            sg = stg.next()
            S.op("act", lambda e: e.activation(out=sg.t[:], in_=pT.t[:], func=AF.Copy), reads=[pT.res], writes=[sg.res])
            for h in range(4):
                S.dma("sp", self.fm[QD0 + h * 96 + 64:QD0 + h * 96 + 96, tb4 * 512:(tb4 + 1) * 512], sg.t[h * 32:(h + 1) * 32, :], reads=[sg.res])


Prog.phase_moba_prep = phase_moba_prep


NIT = 13
U8 = mybir.dt.uint8
NDVE = 8


def phase_dsa(self, l):
    S = self.S
    T, NB = self.T, self.NB
    with ExitStack() as c:
        ki = Buf(self.sb(c, "ds_ki", [32, T], BF16))
        Kb = [Buf(self.sb(c, f"ds_K{i}", [128, T], BF16)) for i in range(2)]
        Vb = Buf(self.sb(c, "ds_V", [128, NB, 4, 65], BF16))
        accr = self.sbring(c, "ds_acc", [128, T], F32, 2)
        mb = Buf(self.sb(c, "ds_mb", [128, T], BF16))
        junk = Buf(self.sb(c, "ds_junk", [128, T], U8))
        tri = Buf(self.sb(c, "ds_tri", [128, 128], F32))
        powt = Buf(self.sb(c, "ds_pow", [128, NIT + 1], F32))
        ones_row = Buf(self.sb(c, "ds_ones", [128, 64], F32))
        rrow = Buf(self.sb(c, "ds_rrow", [128, 512], F32))
        qi_r = self.sbring(c, "ds_qi", [32, 8, 128], BF16, 2)
        qb_r = [self.sbring(c, f"ds_qb{h}", [128, 128], BF16, 3) for h in range(4)]
        dg_r = self.sbring(c, "ds_dg", [128, 4, 128], F32, 1)
        rr = self.sbring(c, "ds_r", [128, 512], F32, 4)
        pt = self.sbring(c, "ds_pt", [128, 512], BF16, 5)
        bcs = self.sbring(c, "ds_bcs", [64, 512], F32, 1)
        ost = self.sbring(c, "ds_ost", [64, 512], BF16, 2)
        sm = self.sbring(c, "ds_sm", [128, 16], F32, 2)
        wt = self.sbring(c, "ds_wt", [128, NIT + 1], F32, 2)
        m8 = self.sbring(c, "ds_m8", [128, 8], F32, 2)
        psX = self.psring(c, "ds_psX", [128, 512], F32, 2)
        psS = self.psring(c, "ds_psS", [128, 512], F32, 4)
        psO = self.psring(c, "ds_psO", [128, 512], F32, 1)
        psB = self.psring(c, "ds_psB", [128, 512], F32, 1)
        S.dma("sp", ki.t[:, :], self.fm[KI0:KI0 + 32, :], writes=[ki.res])
        for i in range(2):
            S.dma("sp", Kb[i].t[:, :], self.fm[KB0 + i * 128:KB0 + (i + 1) * 128, :], writes=[Kb[i].res])
        for h in range(4):
            for b in qb_r[h].bufs:
                S.op("pool", lambda e, b=b: e.memset(b.t[:], 0.0), writes=[b.res])
        for h in range(4):
            S.dma("sp", Vb.t[:, :, h, 0:64], self.vtm.rearrange("(nb p) c -> p nb c", p=128)[:, :, 256 + h * 64:320 + h * 64],
                  writes=[Vb.res])
        S.op("pool", lambda e: e.memset(Vb.t[:, :, :, 64:65], 1.0), writes=[Vb.res])
        S.dma("sp", tri.t[:], self.cin["tri"][:, :], writes=[tri.res])
        S.dma("sp", powt.t[:], self.cin["pow2"][:, 0:NIT + 1], writes=[powt.res])
        S.op("dve", lambda e: e.memset(ones_row.t[:], 1.0), writes=[ones_row.res])
        st = {}

        def index_scores(tb):
            Sk = (tb + 1) * 128
            ts = slice(tb * 128, (tb + 1) * 128)
            qi = qi_r.next()
            S.dma("sp", qi.t[:], self.fm[QI0:QI0 + 256, ts].rearrange("(h e) t -> e h t", e=32), writes=[qi.res])
            qb = [qb_r[h].next() for h in range(4)]
            for h in range(4):
                hr = (h % 2) * 64
                S.dma("sp", qb[h].t[hr:hr + 64, :], self.fm[QB0 + h * 64:QB0 + (h + 1) * 64, ts], writes=[qb[h].res])
            st[tb] = qb
            acc = accr.next()
            st[(tb, "acc")] = acc
            dg = dg_r.next()
            for h in range(max(NDVE, 4), 8):
                S.op("pool", lambda e, h=h: e.tensor_scalar(out=dg.t[:, h - 4, :], in0=self.ident_f.t[:], scalar1=self.witm.t[:, tb, h:h + 1],
                                                            scalar2=None, op0=ALU.mult),
                     reads=[self.ident_f.res, self.witm.res], writes=[dg.res])
            items = [(kc, h) for kc in range((Sk + 511) // 512) for h in range(8)]
            pend = {}
            accp = {}

            def s1(it, kc, h):
                w = min(512, Sk - kc * 512)
                px = psX.next()
                S.op("pe", lambda e: e.matmul(px.t[:, 0:w], lhsT=qi.t[:, h, :], rhs=ki.t[:, kc * 512:kc * 512 + w], start=True, stop=True),
                     reads=[qi.res, ki.res], writes=[px.res])
                r = rr.next()
                S.op("act", lambda e: e.activation(out=r.t[:, 0:w], in_=px.t[:, 0:w], func=AF.Relu), reads=[px.res], writes=[r.res])
                pend[it] = r

            def s2(it, kc, h):
                w = min(512, Sk - kc * 512)
                ks = slice(kc * 512, kc * 512 + w)
                r = pend.pop(it)
                if h == 0:
                    S.op("dve", lambda e: e.tensor_scalar(out=acc.t[:, ks], in0=r.t[:, 0:w], scalar1=self.witm.t[:, tb, 0:1],
                                                          scalar2=None, op0=ALU.mult),
                         reads=[r.res, self.witm.res], writes=[acc.res])
                elif h < NDVE:
                    S.op("dve", lambda e: e.scalar_tensor_tensor(out=acc.t[:, ks], in0=r.t[:, 0:w], scalar=self.witm.t[:, tb, h:h + 1],
                                                                 in1=acc.t[:, ks], op0=ALU.mult, op1=ALU.add),
                         reads=[r.res, self.witm.res, acc.res], writes=[acc.res])
                else:
                    if h == NDVE:
                        accp[kc] = psA.next()
                    pa = accp[kc]
                    S.op("pe", lambda e: e.matmul(pa.t[:, 0:w], lhsT=dg.t[:, h - 4, :], rhs=r.t[:, 0:w], start=(h == NDVE), stop=(h == 7)),
                         reads=[dg.res, r.res], writes=[pa.res], signal=(h == 7))
                    if h == 7:
                        S.op("dve", lambda e: e.tensor_tensor(out=acc.t[:, ks], in0=pa.t[:, 0:w], in1=acc.t[:, ks], op=ALU.add),
                             reads=[pa.res, acc.res], writes=[acc.res])

            for it in range(len(items) + 1):
                if it < len(items):
                    s1(it, *items[it])
                if it >= 1:
                    s2(it - 1, *items[it - 1])
                    yield

        def threshold_mask(tb):
            Sk = (tb + 1) * 128
            ts = slice(tb * 128, (tb + 1) * 128)
            s = sm.next()
            acc = st.pop((tb, "acc"))
            if tb >= 2:
                mx = m8.next()
                S.op("dve", lambda e: e.max(out=mx.t[:], in_=acc.t[:, 0:Sk]), reads=[acc.res], writes=[mx.res])
                S.op("dve", lambda e: e.tensor_reduce(out=s.t[:, 0:1], in_=acc.t[:, 0:Sk], axis=AX.X, op=ALU.min),
                     reads=[acc.res], writes=[s.res])
                S.op("dve", lambda e: e.tensor_tensor(out=s.t[:, 1:2], in0=mx.t[:, 0:1], in1=s.t[:, 0:1], op=ALU.subtract),
                     reads=[mx.res, s.res], writes=[s.res])
                S.op("dve", lambda e: e.tensor_scalar(out=s.t[:, 1:2], in0=s.t[:, 1:2], scalar1=1.0001, scalar2=1e-3, op0=ALU.mult, op1=ALU.add),
                     reads=[s.res], writes=[s.res])
                wtt = wt.next()
                S.op("dve", lambda e: e.tensor_scalar(out=wtt.t[:], in0=powt.t[:], scalar1=s.t[:, 1:2], scalar2=None, op0=ALU.mult),
                     reads=[powt.res, s.res], writes=[wtt.res])
                S.op("dve", lambda e: e.tensor_tensor(out=s.t[:, 2:3], in0=s.t[:, 0:1], in1=wtt.t[:, 0:1], op=ALU.add),
                     reads=[s.res, wtt.res], writes=[s.res])
            S.op("dve", lambda e: e.tensor_tensor(out=acc.t[:, ts], in0=acc.t[:, ts], in1=tri.t[:], op=ALU.add),
                 reads=[acc.res, tri.res], writes=[acc.res])
            if tb >= 2:
                for it in range(NIT):
                    S.op("dve", lambda e: e.tensor_scalar(out=junk.t[:, 0:Sk], in0=acc.t[:, 0:Sk], scalar1=s.t[:, 2:3], scalar2=None,
                                                          op0=ALU.is_ge, op1=ALU.add, accum_out=s.t[:, 3:4]),
                         reads=[acc.res, s.res], writes=[junk.res, s.res])
                    yield
                    S.op("dve", lambda e, it=it: e.scalar_tensor_tensor(out=s.t[:, 4:5], in0=s.t[:, 3:4], scalar=255.5, in1=wtt.t[:, it:it + 1],
                                                                        op0=ALU.is_ge, op1=ALU.mult),
                         reads=[s.res, wtt.res], writes=[s.res])
                    S.op("dve", lambda e, it=it: e.scalar_tensor_tensor(out=s.t[:, 2:3], in0=s.t[:, 2:3], scalar=wtt.t[:, it + 1:it + 2],
                                                                        in1=s.t[:, 4:5], op0=ALU.subtract, op1=ALU.add),
                         reads=[s.res, wtt.res], writes=[s.res])
                S.op("dve", lambda e: e.tensor_tensor(out=s.t[:, 5:6], in0=s.t[:, 2:3], in1=wtt.t[:, NIT:NIT + 1], op=ALU.subtract),
                     reads=[s.res, wtt.res], writes=[s.res])
            else:
                S.op("dve", lambda e: e.memset(s.t[:, 5:6], -1e29), writes=[s.res])
            yield "final"
            S.op("dve", lambda e: e.tensor_scalar(out=mb.t[:, 0:Sk], in0=acc.t[:, 0:Sk], scalar1=s.t[:, 5:6], scalar2=NEG,
                                                  op0=ALU.is_lt, op1=ALU.mult),
                 reads=[acc.res, s.res], writes=[mb.res])

        def attention(tb):
            qb = st[tb]
            po = psO.next()
            st[(tb, "po")] = po
            items = [(h, gi) for h in range(4) for gi in range((tb + 4) // 4)]
            pend = {}

            def s1(it, h, gi):
                j0 = gi * 4
                nb = min(4, tb + 1 - j0)
                ps = psS.next()
                for jj in range(nb):
                    j = j0 + jj
                    S.op("pe", lambda e, j=j, jj=jj: e.matmul(ps.t[:, jj * 128:(jj + 1) * 128], lhsT=Kb[h // 2].t[:, j * 128:(j + 1) * 128],
                                                              rhs=qb[h].t[:, :], start=True, stop=False),
                         reads=[Kb[h // 2].res, qb[h].res], writes=[ps.res], signal=False)
                    S.op("pe", lambda e, j=j, jj=jj: e.matmul(ps.t[:, jj * 128:(jj + 1) * 128], lhsT=mb.t[:, j * 128:(j + 1) * 128],
                                                              rhs=self.ident_bf.t[:], start=False, stop=True),
                         reads=[mb.res, self.ident_bf.res], writes=[ps.res], signal=(jj == nb - 1))
                p = pt.next()
                S.op("act", lambda e: e.activation(out=p.t[:, 0:nb * 128], in_=ps.t[:, 0:nb * 128], func=AF.Exp),
                     reads=[ps.res], writes=[p.res])
                pend[it] = p

            def s2(it, h, gi):
                j0 = gi * 4
                nb = min(4, tb + 1 - j0)
                p = pend.pop(it)
                for jj in range(nb):
                    j = j0 + jj
                    last = (j == tb)
                    S.op("pe", lambda e, j=j, jj=jj, last=last: e.matmul(po.t[0:65, h * 128:(h + 1) * 128], lhsT=Vb.t[:, j, h, :],
                                                                          rhs=p.t[:, jj * 128:(jj + 1) * 128], start=(j == 0), stop=last),
                         reads=[Vb.res, p.res], writes=[po.res], signal=last)

            ALA = 2
            for it in range(len(items) + ALA):
                if it < len(items):
                    s1(it, *items[it])
                if it >= ALA:
                    s2(it - ALA, *items[it - ALA])
                    yield

        def finalize(tb):
            po = st.pop((tb, "po"))
            st.pop(tb)
            ts = slice(tb * 128, (tb + 1) * 128)
            S.op("dve", lambda e: e.reciprocal(out=rrow.t[64:65, :], in_=po.t[64:65, :]), reads=[po.res], writes=[rrow.res])
            pb = psB.next()
            S.op("pe", lambda e: e.matmul(pb.t[0:64, :], lhsT=ones_row.t[64:65, 0:64], rhs=rrow.t[64:65, :], start=True, stop=True),
                 reads=[ones_row.res, rrow.res], writes=[pb.res])
            bc = bcs.next()
            S.op("act", lambda e: e.activation(out=bc.t[0:64, :], in_=pb.t[0:64, :], func=AF.Copy), reads=[pb.res], writes=[bc.res])
            os_ = ost.next()
            S.op("dve", lambda e: e.tensor_tensor(out=os_.t[0:64, :], in0=po.t[0:64, :], in1=bc.t[0:64, :], op=ALU.mult),
                 reads=[po.res, bc.res], writes=[os_.res])
            for h in range(4):
                row = (4 + h) * 64
                S.dma("sp", self.oT[row:row + 64, ts], os_.t[0:64, h * 128:(h + 1) * 128], reads=[os_.res])

        def drain(g):
            if g is not None:
                for _ in g:
                    pass

        def step(g, k):
            if g is None:
                return None
            try:
                for _ in range(k):
                    next(g)
            except StopIteration:
                return None
            return g

        drain(index_scores(0))
        for tb in range(NB):
            gi = index_scores(tb + 1) if tb + 1 < NB else None
            ga = attention(tb - 1) if tb >= 1 else None
            gt = threshold_mask(tb)
            n_i = 8 * ((tb + 2) * 128 + 511) // 512 if gi is not None else 0
            n_a = 4 * ((tb + 3) // 4) if ga is not None else 0
            rounds = NIT if tb >= 2 else 1
            k_i = -(-n_i // rounds)
            k_a = -(-n_a // rounds)
            while True:
                r = next(gt)
                if r == "final":
                    break
                gi = step(gi, k_i)
                ga = step(ga, k_a)
            drain(gi)
            drain(ga)
            drain(gt)
            if tb >= 1:
                finalize(tb - 1)
        drain(attention(NB - 1))
        finalize(NB - 1)


Prog.phase_dsa = phase_dsa


DIL_PATTERNS = ((128, 1), (512, 4), (2048, 16))


def phase_dil(self, l):
    S = self.S
    T, NB, NCK = self.T, self.NB, self.NC
    with ExitStack() as c:
        Qc = self.sbring(c, "dl_Q", [64, T], BF16, 2)
        Kc = self.sbring(c, "dl_K", [64, T], BF16, 2)
        Vr = [self.sbring(c, f"dl_V{r}", [128, NB, 65], BF16, 2) for (_, r) in DIL_PATTERNS]
        accT = Buf(self.sb(c, "dl_acc", [65, T], F32))
        dmask = Buf(self.sb(c, "dl_mask", [128, 2, 128], BF16))
        ones_row = Buf(self.sb(c, "dl_ones", [128, 64], F32))
        rrow = Buf(self.sb(c, "dl_rrow", [128, 512], F32))
        bcs = self.sbring(c, "dl_bcs", [64, 512], F32, 2)
        ost = self.sbring(c, "dl_ost", [64, 512], BF16, 2)
        pt = self.sbring(c, "dl_pt", [128, 256], BF16, 3)
        psS = self.psring(c, "dl_psS", [128, 512], F32, 3)
        psO = self.psring(c, "dl_psO", [128, 512], F32, 2)
        psB = self.psring(c, "dl_psB", [128, 512], F32, 1)
        S.dma("sp", dmask.t[:], self.cin["dilmask"][:, :, :], writes=[dmask.res])
        S.op("dve", lambda e: e.memset(ones_row.t[:], 1.0), writes=[ones_row.res])
        for ring in Vr:
            for b in ring.bufs:
                S.op("pool", lambda e, b=b: e.memset(b.t[:, :, 64:65], 1.0), writes=[b.res])
        for h in range(4):
            Q, K_ = Qc.next(), Kc.next()
            S.dma("sp", Q.t[:, :], self.fm[QC0 + h * 64:QC0 + (h + 1) * 64, :], writes=[Q.res])
            S.dma("sp", K_.t[:, :], self.fm[KC0 + h * 64:KC0 + (h + 1) * 64, :], writes=[K_.res])
            col = 512 + h * 64
            V = []
            for pi, (wdw, r) in enumerate(DIL_PATTERNS):
                v = Vr[pi].next()
                nn = T // (128 * r)
                for n in range(nn):
                    src = self.vtm[n * 128 * r:(n + 1) * 128 * r, col:col + 64].rearrange("(j rho) c -> j rho c", rho=r)
                    S.dma("sp", v.t[:, n * r:(n + 1) * r, 0:64], src, writes=[v.res])
                V.append(v)
            for pi, (wdw, r) in enumerate(DIL_PATTERNS):
                v = V[pi]
                nn = T // (128 * r)
                items = [(n, rho) for n in range(nn) for rho in range(r)]
                pend = {}

                def s1(it, n, rho, r=r):
                    st_ = n * 128 * r + rho
                    qcols = slice(st_, st_ + 127 * r + 1, r)
                    ps = psS.next()
                    lo = 0 if n >= 1 else 128
                    if n >= 1:
                        pcols = slice(st_ - 128 * r, st_ - r + 1, r)
                        S.op("pe", lambda e: e.matmul(ps.t[:, 0:128], lhsT=K_.t[:, pcols], rhs=Q.t[:, qcols], start=True, stop=False),
                             reads=[K_.res, Q.res], writes=[ps.res], signal=False)
                        S.op("pe", lambda e: e.matmul(ps.t[:, 0:128], lhsT=self.ident_bf.t[:], rhs=dmask.t[:, 0, :], start=False, stop=True),
                             reads=[self.ident_bf.res, dmask.res], writes=[ps.res], signal=False)
                    S.op("pe", lambda e: e.matmul(ps.t[:, 128:256], lhsT=K_.t[:, qcols], rhs=Q.t[:, qcols], start=True, stop=False),
                         reads=[K_.res, Q.res], writes=[ps.res], signal=False)
                    S.op("pe", lambda e: e.matmul(ps.t[:, 128:256], lhsT=self.ident_bf.t[:], rhs=dmask.t[:, 1, :], start=False, stop=True),
                         reads=[self.ident_bf.res, dmask.res], writes=[ps.res])
                    p = pt.next()
                    S.op("act", lambda e: e.activation(out=p.t[:, lo:256], in_=ps.t[:, lo:256], func=AF.Exp), reads=[ps.res], writes=[p.res])
                    pend[it] = p

                def s2(it, n, rho, r=r, pi=pi, v=v):
                    st_ = n * 128 * r + rho
                    qcols = slice(st_, st_ + 127 * r + 1, r)
                    p = pend.pop(it)
                    po = psO.next()
                    if n >= 1:
                        S.op("pe", lambda e: e.matmul(po.t[0:65, 0:128], lhsT=v.t[:, (n - 1) * r + rho, :], rhs=p.t[:, 0:128], start=True, stop=False),
                             reads=[v.res, p.res], writes=[po.res], signal=False)
                    S.op("pe", lambda e: e.matmul(po.t[0:65, 0:128], lhsT=v.t[:, n * r + rho, :], rhs=p.t[:, 128:256], start=(n == 0), stop=True),
                         reads=[v.res, p.res], writes=[po.res])
                    if pi == 0:
                        S.op("act", lambda e: e.activation(out=accT.t[0:65, qcols], in_=po.t[0:65, 0:128], func=AF.Copy),
                             reads=[po.res], writes=[accT.res])
                    else:
                        S.op("dve", lambda e: e.tensor_tensor(out=accT.t[0:65, qcols], in0=po.t[0:65, 0:128], in1=accT.t[0:65, qcols], op=ALU.add),
                             reads=[po.res, accT.res], writes=[accT.res])

                for it in range(len(items) + 1):
                    if it < len(items):
                        s1(it, *items[it])
                    if it >= 1:
                        s2(it - 1, *items[it - 1])
            for qc in range(NCK):
                qs = slice(qc * 512, (qc + 1) * 512)
                S.op("dve", lambda e, qs=qs: e.reciprocal(out=rrow.t[64:65, :], in_=accT.t[64:65, qs]), reads=[accT.res], writes=[rrow.res])
                pb = psB.next()
                S.op("pe", lambda e, pb=pb: e.matmul(pb.t[0:64, :], lhsT=ones_row.t[64:65, 0:64], rhs=rrow.t[64:65, :], start=True, stop=True),
                     reads=[ones_row.res, rrow.res], writes=[pb.res])
                os_ = ost.next()
                S.op("dve", lambda e, pb=pb, os_=os_, qs=qs: e.tensor_tensor(out=os_.t[0:64, :], in0=pb.t[0:64, :], in1=accT.t[0:64, qs], op=ALU.mult),
                     reads=[pb.res, accT.res], writes=[os_.res])
                row = (8 + h) * 64
                S.dma("sp", self.oT[row:row + 64, qs], os_.t[0:64, :], reads=[os_.res])


Prog.phase_dil = phase_dil


class LNCtx:
    def __init__(self, P, c, l, which, nb=2):
        S = P.S
        self.g = Buf(P.sb(c, "ln_g", [128, D], F32))
        self.b = Buf(P.sb(c, "ln_b", [128, D], F32))
        S.dma("sp", self.g.t[:], P.lnp[l, 2 * which, :].partition_broadcast(128), writes=[self.g.res])
        S.dma("sp", self.b.t[:], P.lnp[l, 2 * which + 1, :].partition_broadcast(128), writes=[self.b.res])
        self.xin = P.sbring(c, "ln_x", [128, D], F32, 2 if nb == 1 else 3)
        self.z = P.sbring(c, "ln_z", [128, D], F32, 2 if nb == 1 else 4)
        self.junk = P.sbring(c, "ln_junk", [128, D], BF16, 1)
        self.xn = P.sbring(c, "ln_xn", [128, D], F32, 2)
        self.xo = P.sbring(c, "ln_xo", [128, D], F32, 2)
        self.xb = P.sbring(c, "ln_xb", [128, D], BF16, 2)
        self.st = P.sbring(c, "ln_s", [128, 8], F32, 6)
        self.tp = P.psring(c, "ln_tp", [128, 8, 128], BF16, 1)
        self.tst = P.sbring(c, "ln_tst", [128, 8, 128], BF16, 2)


def _step_all(active):
    alive = []
    for g in active:
        try:
            next(g)
            alive.append(g)
        except StopIteration:
            pass
    active[:] = alive


def _drain_all(active):
    while active:
        _step_all(active)


def emit_res_ln(P, L, tb, py, xsrc, dst, make_xT):
    S = P.S
    rows = slice(tb * 128, (tb + 1) * 128)
    x = L.xin.next()
    S.dma("sp", x.t[:], xsrc[rows, :], writes=[x.res])
    z = L.z.next()
    S.op("dve", lambda e: e.scalar_tensor_tensor(out=z.t[:], in0=x.t[:], scalar=float(ALPHA), in1=py.t[:], op0=ALU.mult, op1=ALU.add),
         reads=[x.res, py.res], writes=[z.res])
    s = L.st.next()
    jk = L.junk.next()
    S.op("act", lambda e: e.activation(out=jk.t[:], in_=z.t[:], func=AF.Identity, accum_out=s.t[:, 0:1]),
         reads=[z.res], writes=[jk.res, s.res])
    S.op("act", lambda e: e.activation(out=jk.t[:], in_=z.t[:], func=AF.Square, accum_out=s.t[:, 1:2]),
         reads=[z.res], writes=[jk.res, s.res])
    yield
    S.op("dve", lambda e: e.tensor_scalar(out=s.t[:, 2:3], in0=s.t[:, 0:1], scalar1=1.0 / D, scalar2=None, op0=ALU.mult),
         reads=[s.res], writes=[s.res])
    S.op("dve", lambda e: e.tensor_tensor(out=s.t[:, 3:4], in0=s.t[:, 2:3], in1=s.t[:, 2:3], op=ALU.mult), reads=[s.res], writes=[s.res])
    S.op("dve", lambda e: e.scalar_tensor_tensor(out=s.t[:, 4:5], in0=s.t[:, 1:2], scalar=1.0 / D, in1=s.t[:, 3:4], op0=ALU.mult, op1=ALU.subtract),
         reads=[s.res], writes=[s.res])
    S.op("dve", lambda e: e.tensor_scalar(out=s.t[:, 4:5], in0=s.t[:, 4:5], scalar1=float(LN_EPS), scalar2=None, op0=ALU.add),
         reads=[s.res], writes=[s.res])
    S.op("act", lambda e: e.activation(out=s.t[:, 5:6], in_=s.t[:, 4:5], func=AF.Sqrt), reads=[s.res], writes=[s.res])
    yield
    S.op("dve", lambda e: e.reciprocal(out=s.t[:, 6:7], in_=s.t[:, 5:6]), reads=[s.res], writes=[s.res])
    S.op("dve", lambda e: e.scalar_tensor_tensor(out=s.t[:, 7:8], in0=s.t[:, 2:3], scalar=-1.0, in1=s.t[:, 6:7], op0=ALU.mult, op1=ALU.mult),
         reads=[s.res], writes=[s.res])
    xn = L.xn.next()
    S.op("act", lambda e: e.activation(out=xn.t[:], in_=z.t[:], func=AF.Identity, scale=s.t[:, 6:7], bias=s.t[:, 7:8]),
         reads=[z.res, s.res], writes=[xn.res])
    yield
    S.op("dve", lambda e: e.tensor_tensor(out=xn.t[:], in0=xn.t[:], in1=L.g.t[:], op=ALU.mult), reads=[xn.res, L.g.res], writes=[xn.res])
    xo = L.xo.next()
    S.op("dve", lambda e: e.tensor_tensor(out=xo.t[:], in0=xn.t[:], in1=L.b.t[:], op=ALU.add), reads=[xn.res, L.b.res], writes=[xo.res])
    S.dma("sp", dst[rows, :], xo.t[:], reads=[xo.res])
    if make_xT:
        xb = L.xb.next()
        S.op("act", lambda e: e.activation(out=xb.t[:], in_=xo.t[:], func=AF.Copy), reads=[xo.res], writes=[xb.res])
        yield
        P.emit_xT(tb, xb, L.tp, L.tst)


def phase_F(self, l, xres_in):
    S = self.S
    T, NB, NCK = self.T, self.NB, self.NC
    with ExitStack() as c:
        wo = Buf(self.sb(c, "F_wo", [128, 8, D], BF16))
        for ec in range(8):
            self.load_cast(c, wo, lambda c0, w, ec=ec: wo.t[:, ec, c0:c0 + w], self.w_o[l, ec * 128:(ec + 1) * 128, :], D, "F")
        L = LNCtx(self, c, l, 0)
        oTc = self.sbring(c, "F_oT", [128, 8, 512], BF16, 2)
        psY = self.psring(c, "F_psY", [128, 1024], F32, 2)
        active = []
        for ck in range(NCK):
            o = oTc.next()
            S.dma("sp", o.t[:], self.oT.rearrange("(ec p) t -> p ec t", p=128)[:, :, ck * 512:(ck + 1) * 512], writes=[o.res])
            for j in range(4):
                tb = ck * 4 + j
                py = psY.next()
                for half in range(2):
                    for ec in range(8):
                        S.op("pe", lambda e, ec=ec, half=half: e.matmul(py.t[:, half * 512:(half + 1) * 512], lhsT=o.t[:, ec, j * 128:(j + 1) * 128],
                                                                        rhs=wo.t[:, ec, half * 512:(half + 1) * 512], start=(ec == 0), stop=(ec == 7)),
                             reads=[o.res, wo.res], writes=[py.res], signal=(ec == 7))
                active.append(emit_res_ln(self, L, tb, py, xres_in, self.xres, True))
                _step_all(active)
        _drain_all(active)


Prog.phase_F = phase_F


def phase_G(self, l, last):
    S = self.S
    T, NB = self.T, self.NB
    CH = 256
    NJ = CH // 128
    with ExitStack() as c:
        wup = Buf(self.sb(c, "G_wup", [128, 8, DFF], BF16))
        wdn = Buf(self.sb(c, "G_wdn", [128, 32, D], BF16))
        for dc in range(8):
            self.load_cast(c, wup, lambda c0, w, dc=dc: wup.t[:, dc, c0:c0 + w], self.w_up[l, dc * 128:(dc + 1) * 128, :], DFF, "G")
        for fc in range(32):
            self.load_cast(c, wdn, lambda c0, w, fc=fc: wdn.t[:, fc, c0:c0 + w], self.w_down[l, fc * 128:(fc + 1) * 128, :], D, "G")
        L = LNCtx(self, c, l, 1, nb=1)
        xTc = self.sbring(c, "G_xT", [128, 8, CH], BF16, 2)
        hT = Buf(self.sb(c, "G_hT", [128, 16, CH], BF16))
        rr = self.sbring(c, "G_r", [128, CH], F32, 2)
        psU = self.psring(c, "G_psU", [128, 512], F32, 2)
        psY = self.psring(c, "G_psY", [128, 1024], F32, NJ)
        dst = self.y if last else self.xres
        active = []
        for ck in range(T // CH):
            xt = xTc.next()
            S.dma("sp", xt.t[:], self.xT.rearrange("(dc p) t -> p dc t", p=128)[:, :, ck * CH:(ck + 1) * CH], writes=[xt.res])
            pys = [psY.next() for _ in range(NJ)]
            for fh in range(2):
                for f16 in range(16):
                    fc = fh * 16 + f16
                    pu = psU.next()
                    for dc in range(8):
                        S.op("pe", lambda e, dc=dc, fc=fc, pu=pu: e.matmul(pu.t[:, 0:CH], lhsT=wup.t[:, dc, fc * 128:(fc + 1) * 128], rhs=xt.t[:, dc, :],
                                                                           start=(dc == 0), stop=(dc == 7)),
                             reads=[wup.res, xt.res], writes=[pu.res], signal=(dc == 7))
                    r = rr.next()
                    S.op("act", lambda e, pu=pu, r=r: e.activation(out=r.t[:], in_=pu.t[:, 0:CH], func=AF.Relu), reads=[pu.res], writes=[r.res])
                    S.op("dve", lambda e, r=r, f16=f16: e.tensor_tensor(out=hT.t[:, f16, :], in0=r.t[:], in1=r.t[:], op=ALU.mult),
                         reads=[r.res], writes=[hT.res])
                    if f16 % 6 == 5:
                        _step_all(active)
                for j in range(NJ):
                    py = pys[j]
                    for half in range(2):
                        for f16 in range(16):
                            fc = fh * 16 + f16
                            S.op("pe", lambda e, fc=fc, f16=f16, half=half, py=py, j=j: e.matmul(
                                py.t[:, half * 512:(half + 1) * 512], lhsT=hT.t[:, f16, j * 128:(j + 1) * 128],
                                rhs=wdn.t[:, fc, half * 512:(half + 1) * 512], start=(fc == 0), stop=(fc == 31)),
                                 reads=[hT.res, wdn.res], writes=[py.res], signal=(f16 == 15))
            _drain_all(active)
            for j in range(NJ):
                active.append(emit_res_ln(self, L, ck * NJ + j, pys[j], self.xres, dst, not last))
        _drain_all(active)


Prog.phase_G = phase_G


_CACHE = {}


def kernel(x, w_in, b_f, w_o, ln1_g, ln1_b, w_up, w_down, ln2_g, ln2_b):
    x = np.asarray(x, np.float32)
    B, T, _ = x.shape
    depth = int(np.asarray(w_in).shape[0])
    key = (T, depth)
    if key not in _CACHE:
        P = Prog(T=T, depth=depth)
        _CACHE[key] = P.build()
    nc = _CACHE[key]
    shared = prep_shared(T, depth, w_in, b_f, w_o, ln1_g, ln1_b, w_up, w_down, ln2_g, ln2_b)
    in_maps = []
    for cid in range(N_CORES):
        m = dict(shared)
        m["x"] = np.ascontiguousarray(x[cid % B])
        in_maps.append(m)
    res = run_bass_kernel_spmd(nc, in_maps, core_ids=list(range(N_CORES)))
    return np.stack([np.asarray(res.results[b]["y"], dtype=np.float32) for b in range(B)], axis=0)
```
